# Optimizing a Trainium2 kernel written in Bass

```python
import jax
import jax.numpy as jnp
from jax import lax
import numpy as np

D_MODEL = 1024
BATCH = 8
SEQ = 2048
DEPTH = 2

HEAD_DIM = 64
ROT_DIM = HEAD_DIM // 4
ROPE_THETA = 500000.0
DIL_GROUPS = ((128, 1), (512, 4), (2048, 16))
HEADS_PER_DIL_GROUP = 2
N_HEADS_A = len(DIL_GROUPS) * HEADS_PER_DIL_GROUP
N_HEADS_B = 6
N_HEADS_M = 4
MOBA_BLOCK = 256
MOBA_TOPK = 3
MOBA_Q_CHUNK = 32
N_MEM = 256
D_FF = 4 * D_MODEL
N_BRANCH = 3
WIDTH_A = N_HEADS_A * HEAD_DIM
WIDTH_A_OUT = HEADS_PER_DIL_GROUP * HEAD_DIM
WIDTH_B = N_HEADS_B * HEAD_DIM
WIDTH_M = N_HEADS_M * HEAD_DIM
D_IN = 3 * WIDTH_A + 3 * WIDTH_B + WIDTH_M + N_BRANCH * D_MODEL
RMS_EPS = 1e-6
NEG_INF = -1e30

kernel_name = 'hybrid_dilated_moba_memory_block'


def rms_norm(x, g):
    xf = x.astype(jnp.float32)
    y = xf * lax.rsqrt(jnp.mean(xf * xf, axis=-1, keepdims=True) + RMS_EPS)
    return (y * g.astype(jnp.float32)).astype(x.dtype)


def partial_rope(x, pos):
    inv_freq = 1.0 / (ROPE_THETA ** (jnp.arange(0, ROT_DIM, 2, dtype=jnp.float32) / ROT_DIM))
    ang = pos.astype(jnp.float32)[:, None] * inv_freq[None, :]
    cos = jnp.cos(ang)[None, :, None, :]
    sin = jnp.sin(ang)[None, :, None, :]
    xr = x[..., :ROT_DIM].astype(jnp.float32)
    x1, x2 = xr[..., :ROT_DIM // 2], xr[..., ROT_DIM // 2:]
    rot = jnp.concatenate([x1 * cos - x2 * sin, x2 * cos + x1 * sin], axis=-1).astype(x.dtype)
    return jnp.concatenate([rot, x[..., ROT_DIM:]], axis=-1)


def dilated_window_attn(q, k, v, window, dilation):
    B, S, H, Dh = q.shape
    n = window // dilation
    span = n * dilation
    Sp = -(-S // span) * span
    L = Sp // dilation
    nblk = L // n

    def to_blocks(t):
        t = jnp.pad(t, ((0, 0), (0, Sp - S), (0, 0), (0, 0)))
        t = t.reshape(B, L, dilation, H, Dh).transpose(0, 2, 3, 1, 4)
        return t.reshape(B, dilation, H, nblk, n, Dh)

    def with_prev(t):
        prev = jnp.pad(t, ((0, 0), (0, 0), (0, 0), (1, 0), (0, 0), (0, 0)))[:, :, :, :-1]
        return jnp.concatenate([prev, t], axis=-2)

    qb = to_blocks(q)
    kk = with_prev(to_blocks(k))
    vv = with_prev(to_blocks(v))
    s = jnp.einsum('brhnqd,brhnkd->brhnqk', qb, kk).astype(jnp.float32)
    i = jnp.arange(n)[:, None]
    j = jnp.arange(2 * n)[None, :]
    band = (j >= i) & (j <= i + n)
    first = (jnp.arange(nblk) == 0)[:, None, None] & (j < n)[None]
    mask = band[None] & ~first
    s = jnp.where(mask, s, NEG_INF)
    mx = jnp.max(s, axis=-1, keepdims=True)
    p = jnp.exp(s - mx)
    den = jnp.sum(p, axis=-1, keepdims=True)
    o = jnp.einsum('brhnqk,brhnkd->brhnqd', (p / den).astype(v.dtype), vv)
    lse = (mx + jnp.log(den))[..., 0]
    o = o.reshape(B, dilation, H, L, Dh).transpose(0, 3, 1, 2, 4).reshape(B, Sp, H, Dh)[:, :S]
    lse = lse.reshape(B, dilation, H, L).transpose(0, 3, 1, 2).reshape(B, Sp, H)[:, :S]
    return o, lse


def dilated_mixer(q, k, v):
    B, S = q.shape[0], q.shape[1]
    outs, lses = [], []
    for g, (win, dil) in enumerate(DIL_GROUPS):
        sl = slice(g * HEADS_PER_DIL_GROUP, (g + 1) * HEADS_PER_DIL_GROUP)
        o, l = dilated_window_attn(q[:, :, sl], k[:, :, sl], v[:, :, sl], win, dil)
        outs.append(o)
        lses.append(l)
    w = jax.nn.softmax(jnp.stack(lses, axis=0), axis=0)
    o = jnp.einsum('gbsh,gbshd->bshd', w, jnp.stack(outs, axis=0).astype(jnp.float32))
    return o.astype(q.dtype).reshape(B, S, WIDTH_A_OUT)


def moba_attn(q, k, v):
    B, S, H, Dh = q.shape
    nB = -(-S // MOBA_BLOCK)
    Sp = nB * MOBA_BLOCK
    padw = ((0, 0), (0, Sp - S), (0, 0), (0, 0))
    qt = jnp.pad(q, padw).transpose(0, 2, 1, 3)
    kt = jnp.pad(k, padw).transpose(0, 2, 1, 3)
    vt = jnp.pad(v, padw).transpose(0, 2, 1, 3)
    kb = kt.reshape(B, H, nB, MOBA_BLOCK, Dh)
    vb = vt.reshape(B, H, nB, MOBA_BLOCK, Dh)
    qpos = jnp.arange(Sp, dtype=jnp.int32)
    qblk = qpos // MOBA_BLOCK
    own = jnp.broadcast_to(qblk[None, None, :, None], (B, H, Sp, 1))
    n_sel = min(MOBA_TOPK, nB - 1)
    if n_sel > 0:
        kmean = jnp.mean(kb.astype(jnp.float32), axis=3)
        gate = jnp.einsum('bhsd,bhnd->bhsn', qt.astype(jnp.float32), kmean)
        past = jnp.arange(nB, dtype=jnp.int32)[None, :] < qblk[:, None]
        _, top_idx = lax.top_k(jnp.where(past, gate, NEG_INF), n_sel)
        top_ok = jnp.take_along_axis(jnp.broadcast_to(past, gate.shape), top_idx, axis=-1)
        blk_idx = jnp.concatenate([top_idx.astype(jnp.int32), own], axis=-1)
        blk_ok = jnp.concatenate([top_ok, jnp.ones_like(own, dtype=bool)], axis=-1)
    else:
        blk_idx = own
        blk_ok = jnp.ones_like(own, dtype=bool)
    n_tot = blk_idx.shape[-1]
    nc = Sp // MOBA_Q_CHUNK
    C = MOBA_Q_CHUNK
    q_c = qt.reshape(B, H, nc, C, Dh).transpose(2, 0, 1, 3, 4)
    idx_c = blk_idx.reshape(B, H, nc, C, n_tot).transpose(2, 0, 1, 3, 4)
    ok_c = blk_ok.reshape(B, H, nc, C, n_tot).transpose(2, 0, 1, 3, 4)
    pos_c = qpos.reshape(nc, C)
    bi = jnp.arange(B)[:, None, None, None]
    hi = jnp.arange(H)[None, :, None, None]
    offs = jnp.arange(MOBA_BLOCK, dtype=jnp.int32)

    def chunk(args):
        qc, ic, okc, pc = args
        ks = kb[bi, hi, ic]
        vs = vb[bi, hi, ic]
        s = jnp.einsum('bhcd,bhcnkd->bhcnk', qc, ks).astype(jnp.float32)
        kpos = ic[..., None] * MOBA_BLOCK + offs
        mask = okc[..., None] & (kpos <= pc[None, None, :, None, None])
        s = jnp.where(mask, s, NEG_INF).reshape(B, H, C, n_tot * MOBA_BLOCK)
        p = jax.nn.softmax(s, axis=-1).reshape(B, H, C, n_tot, MOBA_BLOCK)
        return jnp.einsum('bhcnk,bhcnkd->bhcd', p.astype(vs.dtype), vs)

    o = lax.map(chunk, (q_c, idx_c, ok_c, pos_c))
    o = o.transpose(1, 2, 0, 3, 4).reshape(B, H, Sp, Dh)
    return o.transpose(0, 2, 1, 3)[:, :S]


def memory_attn(q, mem_n, w_mkv):
    B, S = q.shape[0], q.shape[1]
    M = mem_n.shape[1]
    kv = (mem_n @ w_mkv).reshape(B, M, 2, N_HEADS_M, HEAD_DIM)
    s = jnp.einsum('bshd,bmhd->bhsm', q, kv[:, :, 0]).astype(jnp.float32)
    p = jax.nn.softmax(s, axis=-1)
    o = jnp.einsum('bhsm,bmhd->bshd', p.astype(q.dtype), kv[:, :, 1])
    return o.reshape(B, S, WIDTH_M)


def hybrid_layer(x, mem, pos, g_mix, w_in, w_pa, w_pb, w_pm, w_o, g_mem, w_mkv, g_mlp, w_up, w_down):
    B, S, _ = x.shape
    scale = HEAD_DIM ** -0.5
    h = rms_norm(x, g_mix)
    z = h @ w_in
    o1 = 3 * WIDTH_A
    o2 = o1 + 3 * WIDTH_B
    o3 = o2 + WIDTH_M
    qkv_a = z[..., :o1].reshape(B, S, 3, N_HEADS_A, HEAD_DIM)
    qkv_b = z[..., o1:o2].reshape(B, S, 3, N_HEADS_B, HEAD_DIM)
    q_m = z[..., o2:o3].reshape(B, S, N_HEADS_M, HEAD_DIM)
    gates = jax.nn.sigmoid(z[..., o3:].reshape(B, S, N_BRANCH, D_MODEL))
    o_a = dilated_mixer(partial_rope(qkv_a[:, :, 0], pos) * scale,
                        partial_rope(qkv_a[:, :, 1], pos), qkv_a[:, :, 2])
    o_b = moba_attn(partial_rope(qkv_b[:, :, 0], pos) * scale,
                    partial_rope(qkv_b[:, :, 1], pos), qkv_b[:, :, 2]).reshape(B, S, WIDTH_B)
    o_m = memory_attn(q_m * scale, rms_norm(mem, g_mem), w_mkv)
    y = (gates[:, :, 0] * (o_a @ w_pa) + gates[:, :, 1] * (o_b @ w_pb)
         + gates[:, :, 2] * (o_m @ w_pm))
    x = x + y @ w_o
    hm = rms_norm(x, g_mlp)
    x = x + jnp.square(jax.nn.relu(hm @ w_up)) @ w_down
    return x


def setup_inputs(seed: int = 0) -> dict:
    key = jax.random.key(seed)
    ks = jax.random.split(key, 16)

    def nrm(k, shape, fan_in):
        return jax.random.normal(k, shape, jnp.float32) * (fan_in ** -0.5)

    def gain(k, shape):
        return 1.0 + 0.05 * jax.random.normal(k, shape, jnp.float32)

    return {
        'x': jax.random.normal(ks[0], (BATCH, SEQ, D_MODEL), jnp.float32),
        'mem': jax.random.normal(ks[1], (BATCH, N_MEM, D_MODEL), jnp.float32),
        'norm_mix': gain(ks[2], (DEPTH, D_MODEL)),
        'w_in': nrm(ks[3], (DEPTH, D_MODEL, D_IN), D_MODEL),
        'w_proj_a': nrm(ks[4], (DEPTH, WIDTH_A_OUT, D_MODEL), WIDTH_A_OUT),
        'w_proj_b': nrm(ks[5], (DEPTH, WIDTH_B, D_MODEL), WIDTH_B),
        'w_proj_m': nrm(ks[6], (DEPTH, WIDTH_M, D_MODEL), WIDTH_M),
        'w_out': nrm(ks[7], (DEPTH, D_MODEL, D_MODEL), D_MODEL),
        'norm_mem': gain(ks[8], (DEPTH, D_MODEL)),
        'w_mem_kv': nrm(ks[9], (DEPTH, D_MODEL, 2 * WIDTH_M), D_MODEL),
        'norm_mlp': gain(ks[10], (DEPTH, D_MODEL)),
        'w_up': nrm(ks[11], (DEPTH, D_MODEL, D_FF), D_MODEL),
        'w_down': nrm(ks[12], (DEPTH, D_FF, D_MODEL), D_FF),
        'norm_final': gain(ks[13], (D_MODEL,)),
    }


def reference(x, mem, norm_mix, w_in, w_proj_a, w_proj_b, w_proj_m, w_out, norm_mem, w_mem_kv,
              norm_mlp, w_up, w_down, norm_final):
    pos = jnp.arange(x.shape[1], dtype=jnp.int32)
    for l in range(DEPTH):
        x = hybrid_layer(x, mem, pos, norm_mix[l], w_in[l], w_proj_a[l], w_proj_b[l], w_proj_m[l],
                         w_out[l], norm_mem[l], w_mem_kv[l], norm_mlp[l], w_up[l], w_down[l])
    return rms_norm(x, norm_final)
```

```python
from contextlib import ExitStack
import numpy as np
import ml_dtypes
import concourse.bass as bass
import concourse.mybir as mybir
from concourse.bass_utils import run_bass_kernel_spmd

F32 = mybir.dt.float32
BF16 = mybir.dt.bfloat16
ALU = mybir.AluOpType
AF = mybir.ActivationFunctionType
AX = mybir.AxisListType

S = 2048
D = 1024
DEPTH = 2
NT = 4
O1 = 1152
O2 = 2304
O3 = 2560
DIL = (1, 4, 16)
NEG = -30000.0
EPS = 1e-6
NTILE_L = 40
ENGS = ("pe", "act", "dve", "pool", "sp")


class _Op:
    __slots__ = ("eng", "fn", "deps", "is_dma", "chan", "tick", "needed")

    def __init__(self, eng, fn, is_dma=False, chan=None):
        self.eng = eng
        self.fn = fn
        self.deps = set()
        self.is_dma = is_dma
        self.chan = chan
        self.tick = None
        self.needed = is_dma


class Sched:
    def __init__(self):
        self.ops = {e: [] for e in ENGS}
        self.last_w = {}
        self.readers = {}
        self.chans = []
        self.bar = []
        self.last_chan = {}

    def _dep(self, op, on, raw):
        if on is op:
            return
        if on.eng == op.eng and not on.is_dma:
            if op.eng in ("pe", "sp"):
                return
        op.deps.add(on)
        on.needed = True

    def _add(self, op, reads, writes):
        for b in self.bar:
            self._dep(op, b, False)
        for k in reads:
            w = self.last_w.get(k)
            if w is not None:
                self._dep(op, w, True)
        for k in writes:
            w = self.last_w.get(k)
            if w is not None:
                self._dep(op, w, False)
            for r in self.readers.get(k, ()):
                self._dep(op, r, False)
        for k in writes:
            self.last_w[k] = op
            self.readers[k] = []
        for k in reads:
            self.readers.setdefault(k, []).append(op)
        self.ops[op.eng].append(op)
        return op

    def op(self, eng, fn, reads=(), writes=()):
        extra = [("psx", k[1]) for k in reads if isinstance(k, tuple) and k[0] == "ps"]
        if extra:
            writes = list(writes) + extra
        return self._add(_Op(eng, fn), reads, writes)

    def dma(self, eng, fn, chan, reads=(), writes=()):
        if chan not in self.chans:
            self.chans.append(chan)
        o = _Op(eng, fn, True, chan)
        self.last_chan[chan] = o
        return self._add(o, reads, writes)

    def barrier(self):
        bar = []
        for e in ENGS:
            for o in reversed(self.ops[e]):
                if not o.is_dma:
                    bar.append(o)
                    break
        bar += list(self.last_chan.values())
        self.bar = bar

    def emit(self, nc):
        with ExitStack() as es:
            esem = {e: es.enter_context(nc.semaphore("s_" + e)) for e in ENGS if e != "sp"}
            csem = {c: es.enter_context(nc.semaphore("c_" + str(c))) for c in self.chans}
            cnt = {e: 0 for e in ENGS}
            ccnt = {c: 0 for c in self.chans}
            for e in ENGS:
                for o in self.ops[e]:
                    if o.is_dma:
                        ccnt[o.chan] += 16
                        o.tick = ccnt[o.chan]
                    elif o.needed:
                        cnt[e] += 1
                        o.tick = cnt[e]
            block = es.enter_context(nc.Block())

            def run(e):
                def body(eng):
                    waited = {}
                    for o in self.ops[e]:
                        need = {}
                        for d in o.deps:
                            key = ("c", d.chan) if d.is_dma else ("e", d.eng)
                            if d.tick > need.get(key, 0):
                                need[key] = d.tick
                        for key, t in need.items():
                            if waited.get(key, 0) >= t:
                                continue
                            waited[key] = t
                            eng.wait_ge(csem[key[1]] if key[0] == "c" else esem[key[1]], t)
                        ins = o.fn(eng)
                        if o.is_dma:
                            ins.then_inc(csem[o.chan], 16)
                        elif o.needed:
                            ins.then_inc(esem[e], 1)
                    if e == "sp":
                        for c in self.chans:
                            eng.wait_ge(csem[c], ccnt[c])
                        for e2 in ENGS:
                            if e2 != "sp" and cnt[e2]:
                                eng.wait_ge(esem[e2], cnt[e2])
                return body

            block.tensor(run("pe"))
            block.scalar(run("act"))
            block.vector(run("dve"))
            block.gpsimd(run("pool"))
            block.sync(run("sp"))


def _kin(w, cols):
    n = len(cols)
    t = np.zeros((128, 4096), np.float32)
    sub = w[:, cols].reshape(8, 128, n).transpose(1, 0, 2).reshape(128, 8 * n)
    t[:, :8 * n] = sub
    return t


def _swap_cols(base):
    idx = []
    for h in range(2):
        b = base + h * 64
        idx += list(range(b + 8, b + 16)) + list(range(b, b + 8)) + list(range(b + 16, b + 64))
    return idx


def _build_wstream(inp):
    tiles = []
    for l in range(DEPTH):
        w_in = np.asarray(inp["w_in"][l], np.float32)
        lt = [_kin(np.asarray(inp["w_mem_kv"][l], np.float32), list(range(512)))]
        for mb in (0, O1):
            for g in range(3):
                lt.append(_kin(w_in, list(range(mb + 768 + g * 128, mb + 768 + (g + 1) * 128))))
                q0 = mb + g * 128
                k0 = mb + 384 + g * 128
                cols = list(range(q0, q0 + 128)) + list(range(k0, k0 + 128))
                lt.append(_kin(w_in, cols))
        lt.append(_kin(w_in, list(range(O2, O3))))
        pcat = np.concatenate([inp["w_proj_a"][l], inp["w_proj_b"][l], inp["w_proj_m"][l]], axis=0).astype(np.float32)
        for c in range(8):
            t = np.zeros((128, 4096), np.float32)
            cols = []
            for i in range(3):
                cols += list(range(O3 + i * 1024 + c * 128, O3 + i * 1024 + (c + 1) * 128))
            t[:, :3072] = w_in[:, cols].reshape(8, 128, 384).transpose(1, 0, 2).reshape(128, 3072)
            t[:, 3072:3840] = pcat[:, c * 128:(c + 1) * 128].reshape(6, 128, 128).transpose(1, 0, 2).reshape(128, 768)
            lt.append(t)
        w_o = np.asarray(inp["w_out"][l], np.float32)
        for hh in range(2):
            lt.append(_kin(w_o, list(range(hh * 512, (hh + 1) * 512))))
        w_up = np.asarray(inp["w_up"][l], np.float32)
        w_dn = np.asarray(inp["w_down"][l], np.float32)
        for g in range(8):
            lt.append(_kin(w_up, list(range(g * 512, (g + 1) * 512))))
            lt.append(w_dn[g * 512:(g + 1) * 512, :].reshape(4, 128, 1024).transpose(1, 0, 2).reshape(128, 4096).copy())
        assert len(lt) == NTILE_L
        tiles += lt
    return np.ascontiguousarray(np.stack(tiles, 0))


def _consts():
    bf = ml_dtypes.bfloat16
    ident = np.eye(128, dtype=np.float32)
    ones = np.ones((128, 128), np.float32)
    ik = np.arange(128)[:, None]
    qq = np.arange(256)[None, :]
    okA = np.where(qq < 128, ik <= qq, ik >= qq - 128)
    maskA = np.where(okA, 0.0, NEG).astype(np.float32)
    caus = np.where(ik <= np.arange(128)[None, :], 0.0, NEG).astype(np.float32)
    pm = np.zeros((128, 128), np.float32)
    for m in range(128):
        j = m % 64
        k = m + 8 if j < 8 else (m - 8 if j < 16 else m)
        pm[k, m] = 1.0
    cb = np.concatenate([ident, ones, maskA, caus, pm], axis=1).astype(bf)
    inv = (1.0 / (np.float32(500000.0) ** (np.arange(0, 16, 2, dtype=np.float32) / np.float32(16)))).astype(np.float32)
    ang = (np.arange(S, dtype=np.float32)[None, :] * inv[:, None]).astype(np.float32)
    cos = np.cos(ang).astype(np.float32)
    sin = np.sin(ang).astype(np.float32)
    C = np.ones((128, S), np.float32)
    Sg = np.zeros((128, S), np.float32)
    for h in range(2):
        C[h * 64:h * 64 + 8] = cos
        C[h * 64 + 8:h * 64 + 16] = cos
        Sg[h * 64:h * 64 + 8] = -sin
        Sg[h * 64 + 8:h * 64 + 16] = sin
    rope = np.stack([C, Sg], 1).astype(bf)
    kaug = np.zeros((8, S), np.float32)
    for c in range(8):
        kaug[c, c * 256:(c + 1) * 256] = -NEG
    return cb, rope, kaug.astype(bf)


TILE_N = [4096] + [1024, 2048] * 6 + [2048] + [3840] * 8 + [4096] * 2 + [4096] * 16


class _Stop(Exception):
    pass


def build_nc(dbg=None):
    nc = bass.Bass("TRN2", target_bir_lowering=False)
    xT_d = nc.dram_tensor("xT", [D, S], F32, kind="ExternalInput").ap()
    memT_d = nc.dram_tensor("memT", [D, 256], F32, kind="ExternalInput").ap()
    ws_d = nc.dram_tensor("wstream", [DEPTH * NTILE_L, 128, 4096], F32, kind="ExternalInput").ap()
    g_d = nc.dram_tensor("gcols", [128, 56], F32, kind="ExternalInput").ap()
    cb_d = nc.dram_tensor("cb", [128, 768], BF16, kind="ExternalInput").ap()
    rope_d = nc.dram_tensor("rope", [128, 2, S], BF16, kind="ExternalInput").ap()
    kaug_d = nc.dram_tensor("kaug", [8, S], BF16, kind="ExternalInput").ap()
    out_d = nc.dram_tensor("outT", [D, S], F32, kind="ExternalOutput").ap()
    if dbg:
        dbg32 = nc.dram_tensor("dbg32", [128, 16384], F32, kind="ExternalOutput").ap()
        dbg16 = nc.dram_tensor("dbg16", [128, 16384], BF16, kind="ExternalOutput").ap()

    s = Sched()
    with ExitStack() as es:
        def sb(name, shape, dt):
            return es.enter_context(nc.sbuf_tensor(name, shape, dt))

        xT = sb("xT_sb", [128, 8, S], F32)
        hT = sb("hT", [128, 8, S], BF16)
        rope = sb("rope_sb", [128, 2, S], BF16)
        cb = sb("cb_sb", [128, 768], BF16)
        gcol = sb("gcol", [128, 56], F32)
        NSLOT = 3
        wring = [sb("wr%d" % i, [128, 4096], BF16) for i in range(NSLOT)]
        qk = sb("qk", [128, 4, S], BF16)
        vt = sb("vt", [128, 16, 2, 128], BF16)
        vm = sb("vm", [128, 2, 4, 128], BF16)
        kmh = sb("kmh", [128, 4, 256], BF16)
        NPT = 5
        pT = [sb("pT%d" % i, [128, 512], BF16) for i in range(NPT)]
        oT = sb("oT", [128, 6, S], BF16)
        tmpA = [sb("tmpA%d" % i, [128, 512], F32) for i in range(2)]
        tmpB = [sb("tmpB%d" % i, [128, 512], F32) for i in range(2)]
        tmpC = [sb("tmpC%d" % i, [128, 512], F32) for i in range(2)]
        sq = [sb("sq%d" % i, [128, 512], BF16) for i in range(2)]
        gsb2 = [sb("gsb%d" % i, [128, 16, 8], F32) for i in range(2)]
        m8 = sb("m8", [128, 8], F32)
        km2 = [sb("km%d" % i, [128, 8], F32) for i in range(2)]
        kmb2 = [sb("kmb%d" % i, [128, 8], BF16) for i in range(2)]
        NBT = 16
        bt = [sb("bt%d" % i, [128, 72], BF16) for i in range(NBT)]
        banks = [es.enter_context(nc.psum_tensor("ps%d" % i, [128, 512], F32)) for i in range(8)]

        qh = [qk[:, 0, :], qk[:, 1, :]]
        kh = [qk[:, 2, :], qk[:, 3, :]]
        qkflat = qk[:].rearrange("p a t -> p (a t)")
        oflat = oT[:].rearrange("p c t -> p (c t)")
        accA = oflat[:, 2048:2048 + 8192].bitcast(F32).rearrange("p (h t) -> p h t", h=2)
        yT = qkflat.rearrange("p (c t) -> p c t", c=8)
        uT = [qkflat[:, i * 2048:(i + 1) * 2048].rearrange("p (j t) -> p j t", j=4) for i in range(2)]
        rT = [qkflat[:, 4096 + i * 512:4096 + (i + 1) * 512] for i in range(2)]
        ostage = qkflat.bitcast(F32)
        mkvw = oflat[:, 0:4096].rearrange("p (k c) -> p k c", k=8)
        memn = oflat[:, 4096:6144].rearrange("p (c t) -> p c t", c=8)
        memT = oflat[:, 6144:10240].bitcast(F32).rearrange("p (c t) -> p c t", c=8)
        VKEYS = [("v", i) for i in range(16)] + ["vones"]

        ident = cb[:, 0:128]
        ones = cb[:, 128:256]
        maskA = cb[:, 256:512]
        caus = cb[:, 512:640]
        pmat = cb[:, 640:768]

        ring = {"S": [0, 1, 2, 3], "O": [4, 5, 6, 7], "P": [4, 5], "G": [6, 7], "D": [4, 5, 6, 7], "S6": [0, 1, 2, 3, 6, 7], "N": [7], "D3": [4, 5, 6]}
        rpos = {"S": 0, "O": 0, "P": 0, "G": 0, "D": 0, "S6": 0, "N": 0, "D3": 0}

        def bank(r):
            b = ring[r][rpos[r] % len(ring[r])]
            rpos[r] += 1
            return b

        cnt = {}

        def nxt(name, n):
            v = cnt.get(name, 0)
            cnt[name] = v + 1
            return v % n

        def MM(out, lhsT, rhs, start, stop, reads, writes, skip=False):
            if skip:
                s.op("pe", lambda e: e.matmul(out, lhsT=lhsT, rhs=rhs, start=start, stop=stop, skip_group_check=True), reads, writes)
            else:
                s.op("pe", lambda e: e.matmul(out, lhsT=lhsT, rhs=rhs, start=start, stop=stop), reads, writes)

        def ACT(out, in_, func, reads, writes, scale=None):
            if scale is None:
                s.op("act", lambda e: e.activation(out=out, in_=in_, func=func), reads, writes)
            else:
                s.op("act", lambda e: e.activation(out=out, in_=in_, func=func, scale=scale), reads, writes)

        def TT(eng, out, in0, in1, op, reads, writes):
            s.op(eng, lambda e: e.tensor_tensor(out=out, in0=in0, in1=in1, op=op), reads, writes)

        def TS(out, in0, s1, s2, op0, op1, reads, writes):
            if op1 is None:
                s.op("dve", lambda e: e.tensor_scalar(out=out, in0=in0, scalar1=s1, scalar2=None, op0=op0), reads, writes)
            else:
                s.op("dve", lambda e: e.tensor_scalar(out=out, in0=in0, scalar1=s1, scalar2=s2, op0=op0, op1=op1), reads, writes)

        def STT(out, in0, scalar, in1, op0, op1, reads, writes):
            s.op("dve", lambda e: e.scalar_tensor_tensor(out=out, in0=in0, scalar=scalar, in1=in1, op0=op0, op1=op1), reads, writes)

        def RECIP(out, in_, reads, writes):
            s.op("dve", lambda e: e.reciprocal(out=out, in_=in_), reads, writes)

        def COPY(eng, out, in_, reads, writes):
            s.op(eng, lambda e: e.tensor_copy(out=out, in_=in_), reads, writes)

        def MEMSET(eng, ap, val, reads, writes):
            s.op(eng, lambda e: e.memset(ap, val), reads, writes)

        def DMA(eng, out, in_, chan, reads, writes):
            s.dma(eng, lambda e: e.dma_start(out=out, in_=in_), chan, reads, writes)

        def chk(name, ap, keys):
            if dbg != name:
                return
            s.barrier()
            n = 1
            for d_ in ap.shape[1:]:
                n *= d_
            dst = dbg32 if ap.dtype == F32 else dbg16
            flat = dst[:, 0:n]
            if len(ap.shape) == 3:
                flat = flat.rearrange("p (a b) -> p a b", a=ap.shape[1])
            elif len(ap.shape) == 4:
                flat = flat.rearrange("p (a b c) -> p a b c", a=ap.shape[1], b=ap.shape[2])
            DMA("sp", flat, ap, "out", keys, [])
            raise _Stop()

        wq = []
        for l in range(DEPTH):
            base = l * NTILE_L
            wq += [base + i for i in range(1, 14)]
            for half in range(2):
                wq += [base + 14 + c for c in range(8)]
                wq += [base + 22, base + 23]
            wq += [base + 24 + i for i in range(16)]
        wstate = {"issued": 0, "used": 0}

        def w_issue(after=()):
            i = wstate["issued"]
            if i >= len(wq):
                return
            slot = i % NSLOT
            t = wq[i]
            n = TILE_N[t % NTILE_L]
            DMA("pool", wring[slot][:, 0:n], ws_d[t, :, 0:n], "w%d" % slot, list(after), [("w", slot)])
            wstate["issued"] += 1

        def w_next(expect):
            i = wstate["used"]
            wstate["used"] += 1
            while wstate["issued"] < min(len(wq), i + NSLOT - 1):
                w_issue()
            assert wq[i] % NTILE_L == expect, (wq[i], expect)
            return wring[i % NSLOT], ("w", i % NSLOT)

        DMA("sp", cb[:], cb_d, "c_cb", [], ["cb"])
        DMA("sp", gcol[:], g_d, "c_g", [], ["gcol"])
        DMA("sp", memT, memT_d.rearrange("(c p) t -> p c t", p=128), "mem", [], ["memscr_t"])
        DMA("pool", oflat[:, 0:4096], ws_d[0, :, 0:4096], "wmk", [], ["memscr_w"])
        DMA("sp", rope[:], rope_d, "c_rope", [], ["rope"])
        xv = xT_d.rearrange("(c p) t -> p c t", p=128)
        for tt in range(NT):
            DMA("sp", xT[:, :, tt * 512:(tt + 1) * 512], xv[:, :, tt * 512:(tt + 1) * 512], "x%d" % tt,
                [("xld", tt - 1)] if tt else [], [("x", c, tt) for c in range(8)] + [("xld", tt)])
        MEMSET("pool", vm[:, :, :, 64:128], 1.0, [], ["vmones"])
        for i in range(NBT):
            MEMSET("pool", bt[i][:, 0:64], 0.0, [], [("bt0", i)])
        for i in range(NSLOT):
            w_issue(after=[("xld", 2)])

        def rms_stats(src, src_keys, tt, n):
            sl = slice(tt * 512, tt * 512 + n)
            bk = bank("G")
            for c in range(8):
                si = nxt("sq", 2)
                ACT(sq[si][:, 0:n], src[:, c, sl], AF.Square, src_keys(c, tt), [("sq", si)])
                MM(banks[bk][:, 0:n], ones, sq[si][:, 0:n], c == 0, c == 7, [("sq", si), "cb"], [("ps", bk)])
            ta = nxt("tmpA", 2)
            tb = nxt("tmpB", 2)
            TS(tmpA[ta][:, 0:n], banks[bk][:, 0:n], 1.0 / D, EPS, ALU.mult, ALU.add, [("ps", bk)], [("tmpA", ta)])
            ACT(tmpA[ta][:, 0:n], tmpA[ta][:, 0:n], AF.Ln, [("tmpA", ta)], [("tmpA", ta)])
            ACT(tmpB[tb][:, 0:n], tmpA[ta][:, 0:n], AF.Exp, [("tmpA", ta)], [("tmpB", tb)], scale=-0.5)
            return tb

        def rmsnorm(src, src_keys, dst, dst_keys, gidx, ntok):
            nt = (ntok + 511) // 512
            for tt in range(nt):
                n = min(512, ntok - tt * 512)
                sl = slice(tt * 512, tt * 512 + n)
                tb = rms_stats(src, src_keys, tt, n)
                for c in range(8):
                    STT(dst[:, c, sl], src[:, c, sl], gcol[:, gidx * 8 + c:gidx * 8 + c + 1], tmpB[tb][:, 0:n], ALU.mult, ALU.mult,
                        src_keys(c, tt) + [("tmpB", tb), "gcol"], dst_keys(c, tt))

        from collections import deque
        bgq = deque()

        def bg_step():
            if bgq:
                try:
                    next(bgq[0][1])
                except StopIteration:
                    bgq.popleft()

        def bg_flush(tt=None):
            while bgq and (tt is None or any(t == tt for t, _ in bgq)):
                bg_step()

        ostg = oflat.bitcast(F32)
        ov = out_d.rearrange("(c p) t -> p c t", p=128)

        def norm_gen(tt, gidx, final=False):
            sl = slice(tt * 512, (tt + 1) * 512)
            bk = bank("N")
            for c in range(8):
                si = nxt("sq", 2)
                ACT(sq[si][:, :], xT[:, c, sl], AF.Square, [("x", c, tt)], [("sq", si)])
                MM(banks[bk][:, :], ones, sq[si][:, :], c == 0, c == 7, [("sq", si), "cb"], [("ps", bk)])
                yield
            tb = nxt("tmpB", 2)
            TS(tmpB[tb][:, :], banks[bk][:, :], 1.0 / D, EPS, ALU.mult, ALU.add, [("ps", bk)], [("tmpB", tb)])
            ACT(tmpB[tb][:, :], tmpB[tb][:, :], AF.Ln, [("tmpB", tb)], [("tmpB", tb)])
            ACT(tmpB[tb][:, :], tmpB[tb][:, :], AF.Exp, [("tmpB", tb)], [("tmpB", tb)], scale=-0.5)
            yield
            for c in range(8):
                if final:
                    so = nxt("ost", 12)
                    STT(ostg[:, so * 512:(so + 1) * 512], xT[:, c, sl], gcol[:, 48 + c:49 + c], tmpB[tb][:, :], ALU.mult, ALU.mult,
                        [("x", c, tt), ("tmpB", tb), "gcol"], [("ost", so)])
                    DMA("sp", ov[:, c, sl], ostg[:, so * 512:(so + 1) * 512], "out%d" % so, [("ost", so)], [])
                else:
                    STT(hT[:, c, sl], xT[:, c, sl], gcol[:, gidx * 8 + c:gidx * 8 + c + 1], tmpB[tb][:, :], ALU.mult, ALU.mult,
                        [("x", c, tt), ("tmpB", tb), "gcol"], [("h", c, tt)])
                if c % 2 == 1:
                    yield

        def mem_prep_dma(l):
            DMA("pool", oflat[:, 0:4096], ws_d[l * NTILE_L, :, 0:4096], "wmk", [], ["memscr_w"])
            DMA("sp", memT, memT_d.rearrange("(c p) t -> p c t", p=128), "mem", [], ["memscr_t"])

        def mem_prep_gen(l, issue=True):
            if issue:
                mem_prep_dma(l)
                for _ in range(24):
                    yield
            bk = bank("N")
            for c in range(8):
                si = nxt("sq", 2)
                ACT(sq[si][:, 0:256], memT[:, c, :], AF.Square, ["memscr_t"], [("sq", si)])
                MM(banks[bk][:, 0:256], ones, sq[si][:, 0:256], c == 0, c == 7, [("sq", si), "cb"], [("ps", bk)])
                yield
            tb = nxt("tmpB", 2)
            TS(tmpB[tb][:, 0:256], banks[bk][:, 0:256], 1.0 / D, EPS, ALU.mult, ALU.add, [("ps", bk)], [("tmpB", tb)])
            ACT(tmpB[tb][:, 0:256], tmpB[tb][:, 0:256], AF.Ln, [("tmpB", tb)], [("tmpB", tb)])
            ACT(tmpB[tb][:, 0:256], tmpB[tb][:, 0:256], AF.Exp, [("tmpB", tb)], [("tmpB", tb)], scale=-0.5)
            yield
            gidx = l * 3 + 1
            for c in range(8):
                STT(memn[:, c, :], memT[:, c, :], gcol[:, gidx * 8 + c:gidx * 8 + c + 1], tmpB[tb][:, 0:256], ALU.mult, ALU.mult,
                    ["memscr_t", ("tmpB", tb), "gcol"], ["memscr_n"])
                if c % 4 == 3:
                    yield
            for ch in range(2):
                bk = bank("N")
                for kc in range(8):
                    MM(banks[bk][:, 0:256], mkvw[:, kc, ch * 128:(ch + 1) * 128], memn[:, kc, :], kc == 0, kc == 7, ["memscr_n", "memscr_w"], [("ps", bk)])
                ACT(kmh[0:64, 2 * ch, :], banks[bk][0:64, 0:256], AF.Copy, [("ps", bk)], [("kmh", 2 * ch)])
                COPY("dve", kmh[0:64, 2 * ch + 1, :], banks[bk][64:128, 0:256], [("ps", bk)], [("kmh", 2 * ch + 1)])
                yield
            for mt in range(2):
                bk = bank("N")
                for kc in range(8):
                    MM(banks[bk][:, 0:256], memn[:, kc, mt * 128:(mt + 1) * 128], mkvw[:, kc, 256:512], kc == 0, kc == 7, ["memscr_n", "memscr_w"], [("ps", bk)])
                ACT(vm[:, mt, :, 0:64], banks[bk][:, 0:256].rearrange("p (h d) -> p h d", h=4), AF.Copy, [("ps", bk)], [("vm", mt)])
                yield

        def norm_tile(tt, gidx):
            bgq.append((tt, norm_gen(tt, gidx)))

        def final_tile(tt):
            bgq.append((tt, norm_gen(tt, 0, final=True)))

        xk = lambda c, tt: [("x", c, tt)]
        hk = lambda c, tt: [("h", c, tt)]
        hall = [("h", c, tt) for c in range(8) for tt in range(NT)]

        def dense(bk, n, lhs_list, rhs_list, reads):
            nk = len(lhs_list)
            for kc in range(nk):
                MM(banks[bk][:, 0:n], lhs_list[kc], rhs_list[kc], kc == 0, kc == nk - 1, reads, [("ps", bk)])
            bg_step()

        def produce_qk(wt, wkey, d=1):
            wv = wt[:, 0:2048].rearrange("p (k c) -> p k c", k=8)

            def stage_a(tt, which):
                tsl = slice(tt * 512, (tt + 1) * 512)
                hkeys = [("h", c, tt) for c in range(8)]
                bA = bank("S6")
                dense(bA, 512, [wv[:, kc, which * 128:(which + 1) * 128] for kc in range(8)], [hT[:, kc, tsl] for kc in range(8)], [wkey] + hkeys)
                si = nxt("sq", 2)
                ACT(sq[si][:], banks[bA][:, :], AF.Identity, [("ps", bA)], [("sq", si)])
                return (tt, which, bA, si)

            def stage_b(tt, which, bA, si):
                tsl = slice(tt * 512, (tt + 1) * 512)
                dst, dkey = (qh, "qh") if which == 0 else (kh, "kh")
                bB = bank("P")
                MM(banks[bB][:, :], pmat, sq[si][:], True, True, [("sq", si), "cb"], [("ps", bB)])
                ta = nxt("tmpA", 2)
                tb = nxt("tmpB", 2)
                if d == 1:
                    TT("dve", tmpA[ta][:], banks[bA][:, :], rope[:, 0, tsl], ALU.mult, [("ps", bA), "rope"], [("tmpA", ta)])
                    TT("dve", tmpB[tb][:], banks[bB][:, :], rope[:, 1, tsl], ALU.mult, [("ps", bB), "rope"], [("tmpB", tb)])
                else:
                    pv = lambda ap: ap.rearrange("p (m r) -> p m r", r=d)
                    pw = lambda ap: ap.rearrange("p (r m) -> p m r", r=d)
                    TT("dve", pw(tmpA[ta][:, :]), pv(banks[bA][:, :]), pv(rope[:, 0, tsl]), ALU.mult, [("ps", bA), "rope"], [("tmpA", ta)])
                    TT("dve", pw(tmpB[tb][:, :]), pv(banks[bB][:, :]), pv(rope[:, 1, tsl]), ALU.mult, [("ps", bB), "rope"], [("tmpB", tb)])
                for hd in range(2):
                    rows = slice(hd * 64, (hd + 1) * 64)
                    if d == 1:
                        o_, a_, b_ = dst[hd][0:64, tsl], tmpA[ta][rows, :], tmpB[tb][rows, :]
                    else:
                        m0, m1 = tt * 512 // d, (tt + 1) * 512 // d
                        o_ = dst[hd][0:64, :].rearrange("p (r m) -> p r m", r=d)[:, :, m0:m1]
                        a_ = tmpA[ta][rows, :].rearrange("p (r m) -> p r m", r=d)
                        b_ = tmpB[tb][rows, :].rearrange("p (r m) -> p r m", r=d)
                    TT("pool" if hd == 0 else "dve", o_, a_, b_, ALU.add, [("tmpA", ta), ("tmpB", tb)], [(dkey, hd, tt)])

            units = [(tt, which) for tt in range(NT) for which in range(2)]
            pend = None
            for u in units:
                cur = stage_a(*u)
                if pend is not None:
                    stage_b(*pend)
                pend = cur
            stage_b(*pend)

        def produce_q_plain(wt, wkey, col0):
            wv = wt[:, 0:2048].rearrange("p (k c) -> p k c", k=8)
            for tt in range(NT):
                tsl = slice(tt * 512, (tt + 1) * 512)
                hkeys = [("h", c, tt) for c in range(8)]
                bA = bank("S")
                dense(bA, 512, [wv[:, kc, col0:col0 + 128] for kc in range(8)], [hT[:, kc, tsl] for kc in range(8)], [wkey] + hkeys)
                ACT(qh[0][0:64, tsl], banks[bA][0:64, :], AF.Copy, [("ps", bA)], [("qh", 0, tt)])
                COPY("dve", qh[1][0:64, tsl], banks[bA][64:128, :], [("ps", bA)], [("qh", 1, tt)])

        def produce_v(wt, wkey, tok_slices):
            wv = wt[:, 0:1024].rearrange("p (k c) -> p k c", k=8)
            for t0 in range(0, 16, 4):
                bk = bank("G")
                for ti in range(4):
                    tsl = tok_slices[t0 + ti]
                    for kc in range(8):
                        MM(banks[bk][:, ti * 128:(ti + 1) * 128], hT[:, kc, tsl], wv[:, kc, :], kc == 0, kc == 7, [wkey] + hall, [("ps", bk)])
                ACT(vt[:, t0:t0 + 4, :, 0:64], banks[bk][:, :].rearrange("p (t h d) -> p t h d", t=4, h=2), AF.Copy,
                    [("ps", bk)], [("v", t0 + i) for i in range(4)])

        def finish_o(ob, n, dst_chunk, hd, tok0):
            tc_ = nxt("tmpC", 2)
            ACT(tmpC[tc_][64:128, 0:n], banks[ob][64:128, 0:n], AF.Ln, [("ps", ob)], [("tmpC", tc_)])
            ACT(tmpC[tc_][64:128, 0:n], tmpC[tc_][64:128, 0:n], AF.Exp, [("tmpC", tc_)], [("tmpC", tc_)], scale=-1.0)
            wk = [("o", dst_chunk, hd)] + (["accA"] if 1 <= dst_chunk <= 4 else [])
            TT("dve", oT[hd * 64:(hd + 1) * 64, dst_chunk, tok0:tok0 + n], banks[ob][0:64, 0:n], tmpC[tc_][64:128, 0:n], ALU.mult,
               [("ps", ob), ("tmpC", tc_)], wk)

        def run_tiles(tiles):
            LOOK = 2
            for i in range(len(tiles) + LOOK):
                if i < len(tiles):
                    t = tiles[i]
                    sbk = bank("S")
                    t["s"](sbk)
                    pi = nxt("pT", NPT)
                    n = t["n"]
                    ACT(pT[pi][:, 0:n], banks[sbk][:, 0:n], AF.Exp, [("ps", sbk)], [("pT", pi)], scale=0.125)
                    t["pi"] = pi
                j = i - LOOK
                if 0 <= j < len(tiles):
                    tiles[j]["pv"](tiles[j])

        try:
          for l in range(DEPTH):
              chk('h', hT[:], hall)

              if l == 0:
                  for _ in mem_prep_gen(0, issue=False):
                      pass
              MEMSET("pool", vt[:, :, :, 64:128], 1.0, [], VKEYS)
              if l == 0:
                  rmsnorm(xT, xk, hT, hk, 0, S)
              for g in range(3):
                  d = DIL[g]
                  nblk = 16 // d
                  wV, wVk = w_next(1 + 2 * g)
                  wR, wRk = w_next(2 + 2 * g)
                  toks = []
                  for r in range(d):
                      for j in range(nblk):
                          st = j * 128 * d + r
                          toks.append(slice(st, st + 127 * d + 1, d))
                  produce_qk(wR, wRk, d)
                  produce_v(wV, wVk, toks)
                  if g == 0:
                      chk('qkA0', qk[:], [])
                      chk('vA0', vt[:], [])
                  if g == 1:
                      chk('qkA1', qk[:], [])
                      chk('vA1', vt[:], [])
                  tiles = []
                  for hd in range(2):
                      qkeys = [("qh", hd, tt) for tt in range(NT)]
                      kkeys = [("kh", hd, tt) for tt in range(NT)]
                      for r in range(d):
                          ostate = {}
                          for jk in range(nblk):
                              nq = 256 if jk < nblk - 1 else 128
                              st = r * (S // d) + jk * 128
                              ksl = slice(st, st + 128)
                              qsl = slice(st, st + nq)

                              def sfn(sbk, hd=hd, ksl=ksl, qsl=qsl, nq=nq, qkeys=qkeys, kkeys=kkeys):
                                  MM(banks[sbk][:, 0:nq], kh[hd][0:64, ksl], qh[hd][0:64, qsl], True, True, qkeys + kkeys, [("ps", sbk)])
                                  MM(banks[sbk][:, 0:nq], ident, maskA[:, 0:nq], False, True, ["cb"], [("ps", sbk)], skip=True)

                              def pvfn(t, hd=hd, r=r, jk=jk, d=d, nblk=nblk, ostate=ostate, g=g):
                                  gi = jk % 4
                                  if gi == 0:
                                      ostate["ob"] = bank("O")
                                  ob = ostate["ob"]
                                  vi = r * nblk + jk
                                  osl = banks[ob][:, gi * 128:(gi + 1) * 128]
                                  if jk > 0:
                                      pprev = ostate["prev_pi"]
                                      MM(osl, vt[:, vi - 1, hd, :], pT[pprev][:, 128:256], True, False,
                                         [("pT", pprev), ("v", vi - 1), "vones"], [("ps", ob)], skip=True)
                                  pc = t["pi"]
                                  MM(osl, vt[:, vi, hd, :], pT[pc][:, 0:128], jk == 0, True, [("pT", pc), ("v", vi), "vones"], [("ps", ob)], skip=True)
                                  ostate["prev_pi"] = pc
                                  if gi == 3 or jk == nblk - 1:
                                      j0 = jk - gi
                                      n = (gi + 1) * 128
                                      st0 = j0 * 128 * d + r
                                      asl = slice(st0, st0 + (n - 1) * d + 1, d)
                                      if g == 0:
                                          COPY("dve", accA[:, hd, asl], banks[ob][:, 0:n], [("ps", ob)],
                                               ["accA", "memscr_t", "memscr_n", "memscr_w"])
                                      else:
                                          TT("dve", accA[:, hd, asl], banks[ob][:, 0:n], accA[:, hd, asl], ALU.add, [("ps", ob), "accA"], ["accA"])

                              tiles.append({"s": sfn, "n": nq, "pv": pvfn})
                  run_tiles(tiles)
              def a_final_gen():
                  for hd in range(2):
                      for tt in range(NT):
                          tsl = slice(tt * 512, (tt + 1) * 512)
                          tc_ = nxt("tmpC", 2)
                          ACT(tmpC[tc_][64:128, :], accA[64:128, hd, tsl], AF.Ln, ["accA"], [("tmpC", tc_)])
                          ACT(tmpC[tc_][64:128, :], tmpC[tc_][64:128, :], AF.Exp, [("tmpC", tc_)], [("tmpC", tc_)], scale=-1.0)
                          COPY("dve", tmpC[tc_][0:64, :], tmpC[tc_][64:128, :], [("tmpC", tc_)], [("tmpC", tc_)])
                          TT("dve", oT[hd * 64:(hd + 1) * 64, 0, tsl], accA[0:64, hd, tsl], tmpC[tc_][0:64, :], ALU.mult,
                             ["accA", ("tmpC", tc_)], [("o", 0, hd), "memscr_w"])
                          yield
              bgq.append((-1, a_final_gen()))
              if dbg == 'oA':
                  bg_flush()

              chk('oA', oT[:, 0, :], [])
              for i in range(2):
                  DMA("sp", kh[i][64:72, :], kaug_d, "ka%d" % i, [], [("kaug", i)])
              nat = [slice(t * 128, (t + 1) * 128) for t in range(16)]
              for g in range(3):
                  wV, wVk = w_next(7 + 2 * g)
                  wR, wRk = w_next(8 + 2 * g)
                  produce_qk(wR, wRk)
                  produce_v(wV, wVk, nat)
                  for hd in range(2):
                      kk = [("kh", hd, tt) for tt in range(NT)]
                      s.op("dve", lambda e, hd=hd: e.tensor_reduce(out=km2[hd][0:64, :], in_=kh[hd][0:64, :].rearrange("p (c t) -> p c t", t=256),
                                                                    axis=AX.X, op=ALU.add), kk, [("km", hd)])
                      TS(kmb2[hd][0:64, :], km2[hd][0:64, :], 1.0 / 256, None, ALU.mult, None, [("km", hd)], [("kmb", hd)])
                  for hd in range(2):
                      qk_ = [("qh", hd, tt) for tt in range(NT)]
                      gb = bank("G")
                      for tq in range(8, 16):
                          MM(banks[gb][:, tq * 8:(tq + 1) * 8], qh[hd][0:64, tq * 128:(tq + 1) * 128], kmb2[hd][0:64, :], True, True,
                             qk_ + [("kmb", hd)], [("ps", gb)])
                      COPY("dve", gsb2[hd][:, 8:16, :].rearrange("p a b -> p (a b)"), banks[gb][:, 64:128], [("ps", gb)], [("gsb", hd)])
                      MEMSET("pool", qh[hd][64:72, 0:1024], 0.0, [], [("qa", hd, 0), ("qa", hd, 1)])
                  btidx = {}
                  for hd in range(2):
                      for b in range(4, 8):
                          MEMSET("dve", gsb2[hd][:, 2 * b:2 * b + 2, b:8], 1e30, [("gsb", hd)], [("gsb", hd)])
                      for tq in range(8, 16):
                          b = tq // 2
                          bi = nxt("bt", NBT)
                          btidx[(hd, tq)] = bi
                          s.op("dve", lambda e, tq=tq, hd=hd: e.max(out=m8[:], in_=gsb2[hd][:, tq, :]), [("gsb", hd)], ["m8"])
                          TS(bt[bi][:, 64:72], gsb2[hd][:, tq, :], m8[:, 10 - b:11 - b], 1.0, ALU.is_ge, ALU.subtract, [("gsb", hd), "m8"], [("bt", bi)])

                  def moba_tiles(hd, qts, g=g):
                      tiles = []
                      for qt in qts:
                          ostate = {}
                          last = 4 * qt + 3
                          for kt in range(last + 1):
                              q0 = max(kt * 128, qt * 512)
                              n = (qt + 1) * 512 - q0
                              diag = kt >= 4 * qt

                              def sfn(sbk, hd=hd, kt=kt, q0=q0, n=n, diag=diag, qt=qt):
                                  MM(banks[sbk][:, 0:n], kh[hd][0:72, kt * 128:(kt + 1) * 128], qh[hd][0:72, q0:q0 + n], True, True,
                                     [("kh", hd, kt // 4), ("kaug", hd), ("qh", hd, qt), ("qa", hd, qt)], [("ps", sbk)])
                                  if diag:
                                      MM(banks[sbk][:, 0:128], ident, caus, False, True, ["cb"], [("ps", sbk)], skip=True)

                              def pvfn(t, hd=hd, kt=kt, q0=q0, n=n, qt=qt, last=last, ostate=ostate, g=g):
                                  if kt == 0:
                                      ostate["ob"] = bank("O")
                                  ob = ostate["ob"]
                                  c0 = q0 - qt * 512
                                  pc = t["pi"]
                                  MM(banks[ob][:, c0:512], vt[:, kt, hd, :], pT[pc][:, 0:n], kt == 0, kt == last,
                                     [("pT", pc), ("v", kt), "vones"], [("ps", ob)], skip=True)
                                  if kt == last:
                                      finish_o(ob, 512, 1 + g, hd, qt * 512)

                              tiles.append({"s": sfn, "n": n, "pv": pvfn})
                      return tiles

                  bg_flush()
                  run_tiles(moba_tiles(0, (0, 1)) + moba_tiles(1, (0, 1)))
                  for hd in range(2):
                      for tq4 in (2, 3):
                          tb_ = bank("G")
                          for ti in range(4):
                              bi = btidx[(hd, tq4 * 4 + ti)]
                              MM(banks[tb_][0:72, ti * 128:(ti + 1) * 128], bt[bi][:, 0:72], ident, True, True,
                                 [("bt", bi), ("bt0", bi), "cb"], [("ps", tb_)])
                          ACT(qh[hd][64:72, tq4 * 512:(tq4 + 1) * 512], banks[tb_][64:72, :], AF.Copy, [("ps", tb_)], [("qa", hd, tq4)])
                  run_tiles(moba_tiles(0, (2, 3)) + moba_tiles(1, (2, 3)))

              chk('oB', oT[:, 1:4, :], [])
              wQ, wQk = w_next(13)
              for ch in range(2):
                  produce_q_plain(wQ, wQk, ch * 128)
                  tiles = []
                  for hd in range(2):
                      hm = 2 * ch + hd
                      for qt in range(NT):
                          ostate = {}
                          for mt in range(2):
                              def sfn(sbk, hd=hd, hm=hm, qt=qt, mt=mt):
                                  MM(banks[sbk][:, 0:512], kmh[0:64, hm, mt * 128:(mt + 1) * 128], qh[hd][0:64, qt * 512:(qt + 1) * 512], True, True,
                                     [("kmh", hm), ("qh", hd, qt)], [("ps", sbk)])

                              def pvfn(t, hd=hd, hm=hm, qt=qt, mt=mt, ostate=ostate, ch=ch):
                                  if mt == 0:
                                      ostate["ob"] = bank("O")
                                  ob = ostate["ob"]
                                  pc = t["pi"]
                                  MM(banks[ob][:, 0:512], vm[:, mt, hm, :], pT[pc][:, 0:512], mt == 0, mt == 1,
                                     [("pT", pc), ("vm", mt), "vmones"], [("ps", ob)])
                                  if mt == 1:
                                      finish_o(ob, 512, 4 + ch, hd, qt * 512)

                              tiles.append({"s": sfn, "n": 512, "pv": pvfn})
                  run_tiles(tiles)

              chk('oM', oT[:, 4:6, :], [])
              s.barrier()
              okeys = [("o", ci, hd) for ci in range(6) for hd in range(2)]
              for half in range(2):
                  for c in range(8):
                      wY, wYk = w_next(14 + c)
                      wG = wY[:, 0:3072].rearrange("p (k c) -> p k c", k=8)
                      wP = wY[:, 3072:3840].rearrange("p (k c) -> p k c", k=6)
                      for t2 in range(2):
                          tt = half * 2 + t2
                          tsl = slice(tt * 512, (tt + 1) * 512)
                          ysl = slice(t2 * 512, (t2 + 1) * 512)
                          hkeys = [("h", cc, tt) for cc in range(8)]
                          rhs_h = [hT[:, kc, tsl] for kc in range(8)]
                          prods = []
                          for i, (k0, nk) in enumerate(((0, 1), (1, 3), (4, 2))):
                              bg = bank("S")
                              dense(bg, 512, [wG[:, kc, i * 128:(i + 1) * 128] for kc in range(8)], rhs_h, [wYk] + hkeys)
                              if i == 1:
                                  ti_ = nxt("tmpC", 2)
                                  sg, sgk = tmpC[ti_], ("tmpC", ti_)
                              else:
                                  ti_ = nxt("tmpA", 2)
                                  sg, sgk = tmpA[ti_], ("tmpA", ti_)
                              ACT(sg[:], banks[bg][:, :], AF.Sigmoid, [("ps", bg)], [sgk])
                              bp = bank("S")
                              dense(bp, 512, [wP[:, k0 + kc, :] for kc in range(nk)], [oT[:, k0 + kc, tsl] for kc in range(nk)], [wYk] + okeys)
                              TT("dve", sg[:], banks[bp][:, :], sg[:], ALU.mult, [("ps", bp), sgk], [sgk])
                              prods.append((sg, sgk))
                          TT("pool", prods[0][0][:], prods[0][0][:], prods[1][0][:], ALU.add, [prods[0][1], prods[1][1]], [prods[0][1]])
                          TT("pool", yT[:, c, ysl], prods[0][0][:], prods[2][0][:], ALU.add, [prods[0][1], prods[2][1]], [("y", c, t2)])
                  for oh in range(2):
                      wO, wOk = w_next(22 + oh)
                      wOv = wO[:, 0:4096].rearrange("p (k c) -> p k c", k=8)
                      for cc in range(4):
                          co = oh * 4 + cc
                          for t2 in range(2):
                              tt = half * 2 + t2
                              tsl = slice(tt * 512, (tt + 1) * 512)
                              ysl = slice(t2 * 512, (t2 + 1) * 512)
                              bo = bank("S")
                              dense(bo, 512, [wOv[:, kc, cc * 128:(cc + 1) * 128] for kc in range(8)], [yT[:, kc, ysl] for kc in range(8)],
                                    [wOk] + [("y", kc, t2) for kc in range(8)])
                              TT("dve", xT[:, co, tsl], banks[bo][:, :], xT[:, co, tsl], ALU.add, [("ps", bo), ("x", co, tt)], [("x", co, tt)])
                  for t2 in range(2):
                      norm_tile(half * 2 + t2, l * 3 + 2)
              s.barrier()

              if dbg == 'x1':
                  bg_flush()
              chk('x1', xT[:], [])
              if l + 1 < DEPTH and not dbg:
                  bgq.append((-2, mem_prep_gen(l + 1)))
              seq = [(g, tt) for g in range(8) for tt in range(NT)]
              wst = {}
              prev = None
              for item in seq + [None]:
                  if item is not None:
                      g, tt = item
                      if g == 0:
                          bg_flush(tt)
                      if tt == 0:
                          wU, wUk = w_next(24 + 2 * g)
                          wst["U"] = (wU[:, 0:4096].rearrange("p (k c) -> p k c", k=8), wUk)
                      wUv, wUk = wst["U"]
                      tsl = slice(tt * 512, (tt + 1) * 512)
                      hkeys = [("h", cc, tt) for cc in range(8)]
                      rhs_h = [hT[:, kc, tsl] for kc in range(8)]
                      ui = nxt("u", 2)
                      for j in range(4):
                          bu = bank("S")
                          dense(bu, 512, [wUv[:, kc, j * 128:(j + 1) * 128] for kc in range(8)], rhs_h, [wUk] + hkeys)
                          ri = nxt("r", 2)
                          ACT(rT[ri], banks[bu][:, :], AF.Relu, [("ps", bu)], [("r", ri)])
                          TT("pool", uT[ui][:, j, :], rT[ri], rT[ri], ALU.mult, [("r", ri)], [("u", ui, j)])
                      cur = (g, tt, ui)
                      if g == 7 and tt == NT - 1:
                          w_issue()
                  else:
                      cur = None
                  if prev is not None:
                      g, tt, ui = prev
                      if tt == 0:
                          wD, wDk = w_next(25 + 2 * g)
                          wst["D"] = (wD[:, 0:4096].rearrange("p (j c) -> p j c", j=4), wDk)
                      wDv, wDk = wst["D"]
                      tsl = slice(tt * 512, (tt + 1) * 512)
                      for co in range(8):
                          bd = bank("D3")
                          dense(bd, 512, [wDv[:, jj, co * 128:(co + 1) * 128] for jj in range(4)], [uT[ui][:, jj, :] for jj in range(4)],
                                [wDk] + [("u", ui, jj) for jj in range(4)])
                          TT("dve", xT[:, co, tsl], banks[bd][:, :], xT[:, co, tsl], ALU.add, [("ps", bd), ("x", co, tt)], [("x", co, tt)])
                      if g == 7 and not dbg:
                          if l + 1 < DEPTH:
                              norm_tile(tt, (l + 1) * 3)
                          else:
                              final_tile(tt)
                  prev = cur
              w_issue()
              bg_flush()
              s.barrier()
              chk('x2', xT[:], [])

        except _Stop:
            pass
        else:
          assert wstate["used"] == len(wq), (wstate, len(wq))
        s.emit(nc)
    return nc


_NC_CACHE = {}


def kernel(_dbg=None, **inp):
    x = np.asarray(inp["x"], np.float32)
    mem = np.asarray(inp["mem"], np.float32)
    B = x.shape[0]
    ws = _build_wstream(inp)
    cbn, ropen, kaugn = _consts()
    vecs = []
    for l in range(DEPTH):
        vecs += [inp["norm_mix"][l], inp["norm_mem"][l], inp["norm_mlp"][l]]
    vecs.append(inp["norm_final"])
    gcols = np.concatenate([np.asarray(v, np.float32).reshape(8, 128).T for v in vecs], axis=1)
    gcols = np.ascontiguousarray(gcols)
    if _dbg:
        nc = build_nc(_dbg)
    else:
        if "nc" not in _NC_CACHE:
            _NC_CACHE["nc"] = build_nc()
        nc = _NC_CACHE["nc"]
    in_maps = []
    for b in range(B):
        in_maps.append({
            "xT": np.ascontiguousarray(x[b].T),
            "memT": np.ascontiguousarray(mem[b].T),
            "wstream": ws,
            "gcols": gcols,
            "cb": cbn,
            "rope": ropen,
            "kaug": kaugn,
        })
    res = run_bass_kernel_spmd(nc, in_maps, core_ids=list(range(B)))
    if _dbg:
        return res.results[0]
    out = np.stack([np.ascontiguousarray(r["outT"].T) for r in res.results], 0)
    return out.astype(np.float32)
```

```python
from contextlib import ExitStack
import numpy as np
import ml_dtypes
import concourse.bass as bass
import concourse.mybir as mybir
from concourse.bass_utils import run_bass_kernel_spmd

F32 = mybir.dt.float32
BF16 = mybir.dt.bfloat16
ALU = mybir.AluOpType
AF = mybir.ActivationFunctionType
AX = mybir.AxisListType

S = 2048
D = 1024
DEPTH = 2
NT = 4
O1 = 1152
O2 = 2304
O3 = 2560
DIL = (1, 4, 16)
NEG = -30000.0
EPS = 1e-6
NTILE_L = 40
ENGS = ("pe", "act", "dve", "pool", "sp")


class _Op:
    __slots__ = ("eng", "fn", "deps", "is_dma", "chan", "tick", "needed")

    def __init__(self, eng, fn, is_dma=False, chan=None):
        self.eng = eng
        self.fn = fn
        self.deps = set()
        self.is_dma = is_dma
        self.chan = chan
        self.tick = None
        self.needed = is_dma


class Sched:
    def __init__(self):
        self.ops = {e: [] for e in ENGS}
        self.last_w = {}
        self.readers = {}
        self.chans = []
        self.bar = []
        self.last_chan = {}

    def _dep(self, op, on, raw):
        if on is op:
            return
        if on.eng == op.eng and not on.is_dma:
            if op.eng in ("pe", "sp"):
                return
        op.deps.add(on)
        on.needed = True

    def _add(self, op, reads, writes):
        for b in self.bar:
            self._dep(op, b, False)
        for k in reads:
            w = self.last_w.get(k)
            if w is not None:
                self._dep(op, w, True)
        for k in writes:
            w = self.last_w.get(k)
            if w is not None:
                self._dep(op, w, False)
            for r in self.readers.get(k, ()):
                self._dep(op, r, False)
        for k in writes:
            self.last_w[k] = op
            self.readers[k] = []
        for k in reads:
            self.readers.setdefault(k, []).append(op)
        self.ops[op.eng].append(op)
        return op

    def op(self, eng, fn, reads=(), writes=()):
        extra = [("psx", k[1]) for k in reads if isinstance(k, tuple) and k[0] == "ps"]
        if extra:
            writes = list(writes) + extra
        return self._add(_Op(eng, fn), reads, writes)

    def dma(self, eng, fn, chan, reads=(), writes=()):
        if chan not in self.chans:
            self.chans.append(chan)
        o = _Op(eng, fn, True, chan)
        self.last_chan[chan] = o
        return self._add(o, reads, writes)

    def barrier(self):
        bar = []
        for e in ENGS:
            for o in reversed(self.ops[e]):
                if not o.is_dma:
                    bar.append(o)
                    break
        bar += list(self.last_chan.values())
        self.bar = bar

    def emit(self, nc):
        with ExitStack() as es:
            esem = {e: es.enter_context(nc.semaphore("s_" + e)) for e in ENGS if e != "sp"}
            csem = {c: es.enter_context(nc.semaphore("c_" + str(c))) for c in self.chans}
            cnt = {e: 0 for e in ENGS}
            ccnt = {c: 0 for c in self.chans}
            for e in ENGS:
                for o in self.ops[e]:
                    if o.is_dma:
                        ccnt[o.chan] += 16
                        o.tick = ccnt[o.chan]
                    elif o.needed:
                        cnt[e] += 1
                        o.tick = cnt[e]
            block = es.enter_context(nc.Block())

            def run(e):
                def body(eng):
                    waited = {}
                    for o in self.ops[e]:
                        need = {}
                        for d in o.deps:
                            key = ("c", d.chan) if d.is_dma else ("e", d.eng)
                            if d.tick > need.get(key, 0):
                                need[key] = d.tick
                        for key, t in need.items():
                            if waited.get(key, 0) >= t:
                                continue
                            waited[key] = t
                            eng.wait_ge(csem[key[1]] if key[0] == "c" else esem[key[1]], t)
                        ins = o.fn(eng)
                        if o.is_dma:
                            ins.then_inc(csem[o.chan], 16)
                        elif o.needed:
                            ins.then_inc(esem[e], 1)
                    if e == "sp":
                        for c in self.chans:
                            eng.wait_ge(csem[c], ccnt[c])
                        for e2 in ENGS:
                            if e2 != "sp" and cnt[e2]:
                                eng.wait_ge(esem[e2], cnt[e2])
                return body

            block.tensor(run("pe"))
            block.scalar(run("act"))
            block.vector(run("dve"))
            block.gpsimd(run("pool"))
            block.sync(run("sp"))


def _kin(w, cols):
    n = len(cols)
    t = np.zeros((128, 4096), np.float32)
    sub = w[:, cols].reshape(8, 128, n).transpose(1, 0, 2).reshape(128, 8 * n)
    t[:, :8 * n] = sub
    return t


def _swap_cols(base):
    idx = []
    for h in range(2):
        b = base + h * 64
        idx += list(range(b + 8, b + 16)) + list(range(b, b + 8)) + list(range(b + 16, b + 64))
    return idx


def _build_wstream(inp):
    tiles = []
    for l in range(DEPTH):
        w_in = np.asarray(inp["w_in"][l], np.float32)
        lt = [_kin(np.asarray(inp["w_mem_kv"][l], np.float32), list(range(512)))]
        for mb in (0, O1):
            for g in range(3):
                lt.append(_kin(w_in, list(range(mb + 768 + g * 128, mb + 768 + (g + 1) * 128))))
                q0 = mb + g * 128
                k0 = mb + 384 + g * 128
                cols = list(range(q0, q0 + 128)) + list(range(k0, k0 + 128))
                lt.append(_kin(w_in, cols))
        lt.append(_kin(w_in, list(range(O2, O3))))
        pcat = np.concatenate([inp["w_proj_a"][l], inp["w_proj_b"][l], inp["w_proj_m"][l]], axis=0).astype(np.float32)
        for c in range(8):
            t = np.zeros((128, 4096), np.float32)
            cols = []
            for i in range(3):
                cols += list(range(O3 + i * 1024 + c * 128, O3 + i * 1024 + (c + 1) * 128))
            t[:, :3072] = w_in[:, cols].reshape(8, 128, 384).transpose(1, 0, 2).reshape(128, 3072)
            t[:, 3072:3840] = pcat[:, c * 128:(c + 1) * 128].reshape(6, 128, 128).transpose(1, 0, 2).reshape(128, 768)
            lt.append(t)
        w_o = np.asarray(inp["w_out"][l], np.float32)
        for hh in range(2):
            lt.append(_kin(w_o, list(range(hh * 512, (hh + 1) * 512))))
        w_up = np.asarray(inp["w_up"][l], np.float32)
        w_dn = np.asarray(inp["w_down"][l], np.float32)
        for g in range(8):
            lt.append(_kin(w_up, list(range(g * 512, (g + 1) * 512))))
            lt.append(w_dn[g * 512:(g + 1) * 512, :].reshape(4, 128, 1024).transpose(1, 0, 2).reshape(128, 4096).copy())
        assert len(lt) == NTILE_L
        tiles += lt
    return np.ascontiguousarray(np.stack(tiles, 0))


def _consts():
    bf = ml_dtypes.bfloat16
    ident = np.eye(128, dtype=np.float32)
    ones = np.ones((128, 128), np.float32)
    ik = np.arange(128)[:, None]
    qq = np.arange(256)[None, :]
    okA = np.where(qq < 128, ik <= qq, ik >= qq - 128)
    maskA = np.where(okA, 0.0, NEG).astype(np.float32)
    caus = np.where(ik <= np.arange(128)[None, :], 0.0, NEG).astype(np.float32)
    pm = np.zeros((128, 128), np.float32)
    for m in range(128):
        j = m % 64
        k = m + 8 if j < 8 else (m - 8 if j < 16 else m)
        pm[k, m] = 1.0
    cb = np.concatenate([ident, ones, maskA, caus, pm], axis=1).astype(bf)
    inv = (1.0 / (np.float32(500000.0) ** (np.arange(0, 16, 2, dtype=np.float32) / np.float32(16)))).astype(np.float32)
    ang = (np.arange(S, dtype=np.float32)[None, :] * inv[:, None]).astype(np.float32)
    cos = np.cos(ang).astype(np.float32)
    sin = np.sin(ang).astype(np.float32)
    C = np.ones((128, S), np.float32)
    Sg = np.zeros((128, S), np.float32)
    for h in range(2):
        C[h * 64:h * 64 + 8] = cos
        C[h * 64 + 8:h * 64 + 16] = cos
        Sg[h * 64:h * 64 + 8] = -sin
        Sg[h * 64 + 8:h * 64 + 16] = sin
    rope = np.stack([C, Sg], 1).astype(bf)
    kaug = np.zeros((8, S), np.float32)
    for c in range(8):
        kaug[c, c * 256:(c + 1) * 256] = -NEG
    return cb, rope, kaug.astype(bf)


TILE_N = [4096] + [1024, 2048] * 6 + [2048] + [3840] * 8 + [4096] * 2 + [4096] * 16


class _Stop(Exception):
    pass


def build_nc(dbg=None):
    nc = bass.Bass("TRN2", target_bir_lowering=False)
    xT_d = nc.dram_tensor("xT", [D, S], F32, kind="ExternalInput").ap()
    memT_d = nc.dram_tensor("memT", [D, 256], F32, kind="ExternalInput").ap()
    ws_d = nc.dram_tensor("wstream", [DEPTH * NTILE_L, 128, 4096], F32, kind="ExternalInput").ap()
    g_d = nc.dram_tensor("gcols", [128, 56], F32, kind="ExternalInput").ap()
    cb_d = nc.dram_tensor("cb", [128, 768], BF16, kind="ExternalInput").ap()
    rope_d = nc.dram_tensor("rope", [128, 2, S], BF16, kind="ExternalInput").ap()
    kaug_d = nc.dram_tensor("kaug", [8, S], BF16, kind="ExternalInput").ap()
    out_d = nc.dram_tensor("outT", [D, S], F32, kind="ExternalOutput").ap()
    if dbg:
        dbg32 = nc.dram_tensor("dbg32", [128, 16384], F32, kind="ExternalOutput").ap()
        dbg16 = nc.dram_tensor("dbg16", [128, 16384], BF16, kind="ExternalOutput").ap()

    s = Sched()
    with ExitStack() as es:
        def sb(name, shape, dt):
            return es.enter_context(nc.sbuf_tensor(name, shape, dt))

        xT = sb("xT_sb", [128, 8, S], F32)
        hT = sb("hT", [128, 8, S], BF16)
        rope = sb("rope_sb", [128, 2, S], BF16)
        cb = sb("cb_sb", [128, 768], BF16)
        gcol = sb("gcol", [128, 56], F32)
        NSLOT = 3
        wring = [sb("wr%d" % i, [128, 4096], BF16) for i in range(NSLOT)]
        qk = sb("qk", [128, 4, S], BF16)
        vt = sb("vt", [128, 16, 2, 128], BF16)
        vm = sb("vm", [128, 2, 4, 128], BF16)
        kmh = sb("kmh", [128, 4, 256], BF16)
        NPT = 5
        pT = [sb("pT%d" % i, [128, 512], BF16) for i in range(NPT)]
        oT = sb("oT", [128, 6, S], BF16)
        tmpA = [sb("tmpA%d" % i, [128, 512], F32) for i in range(2)]
        tmpB = [sb("tmpB%d" % i, [128, 512], F32) for i in range(2)]
        tmpC = [sb("tmpC%d" % i, [128, 512], F32) for i in range(2)]
        sq = [sb("sq%d" % i, [128, 512], BF16) for i in range(2)]
        gsb2 = [sb("gsb%d" % i, [128, 16, 8], F32) for i in range(2)]
        m8 = sb("m8", [128, 8], F32)
        km2 = [sb("km%d" % i, [128, 8], F32) for i in range(2)]
        kmb2 = [sb("kmb%d" % i, [128, 8], BF16) for i in range(2)]
        NBT = 16
        bt = [sb("bt%d" % i, [128, 72], BF16) for i in range(NBT)]
        banks = [es.enter_context(nc.psum_tensor("ps%d" % i, [128, 512], F32)) for i in range(8)]

        qh = [qk[:, 0, :], qk[:, 1, :]]
        kh = [qk[:, 2, :], qk[:, 3, :]]
        qkflat = qk[:].rearrange("p a t -> p (a t)")
        oflat = oT[:].rearrange("p c t -> p (c t)")
        accA = oflat[:, 2048:2048 + 8192].bitcast(F32).rearrange("p (h t) -> p h t", h=2)
        yT = qkflat.rearrange("p (c t) -> p c t", c=8)
        uT = [qkflat[:, i * 2048:(i + 1) * 2048].rearrange("p (j t) -> p j t", j=4) for i in range(2)]
        rT = [qkflat[:, 4096 + i * 512:4096 + (i + 1) * 512] for i in range(2)]
        ostage = qkflat.bitcast(F32)
        mkvw = oflat[:, 0:4096].rearrange("p (k c) -> p k c", k=8)
        memn = oflat[:, 4096:6144].rearrange("p (c t) -> p c t", c=8)
        memT = oflat[:, 6144:10240].bitcast(F32).rearrange("p (c t) -> p c t", c=8)
        VKEYS = [("v", i) for i in range(16)] + ["vones"]

        ident = cb[:, 0:128]
        ones = cb[:, 128:256]
        maskA = cb[:, 256:512]
        caus = cb[:, 512:640]
        pmat = cb[:, 640:768]

        ring = {"S": [0, 1, 2, 3], "O": [4, 5, 6, 7], "P": [4, 5], "G": [6, 7], "D": [4, 5, 6, 7], "S6": [0, 1, 2, 3, 6, 7], "N": [7], "D3": [4, 5, 6]}
        rpos = {"S": 0, "O": 0, "P": 0, "G": 0, "D": 0, "S6": 0, "N": 0, "D3": 0}

        def bank(r):
            b = ring[r][rpos[r] % len(ring[r])]
            rpos[r] += 1
            return b

        cnt = {}

        def nxt(name, n):
            v = cnt.get(name, 0)
            cnt[name] = v + 1
            return v % n

        def MM(out, lhsT, rhs, start, stop, reads, writes, skip=False):
            if skip:
                s.op("pe", lambda e: e.matmul(out, lhsT=lhsT, rhs=rhs, start=start, stop=stop, skip_group_check=True), reads, writes)
            else:
                s.op("pe", lambda e: e.matmul(out, lhsT=lhsT, rhs=rhs, start=start, stop=stop), reads, writes)

        def ACT(out, in_, func, reads, writes, scale=None):
            if scale is None:
                s.op("act", lambda e: e.activation(out=out, in_=in_, func=func), reads, writes)
            else:
                s.op("act", lambda e: e.activation(out=out, in_=in_, func=func, scale=scale), reads, writes)

        def TT(eng, out, in0, in1, op, reads, writes):
            s.op(eng, lambda e: e.tensor_tensor(out=out, in0=in0, in1=in1, op=op), reads, writes)

        def TS(out, in0, s1, s2, op0, op1, reads, writes):
            if op1 is None:
                s.op("dve", lambda e: e.tensor_scalar(out=out, in0=in0, scalar1=s1, scalar2=None, op0=op0), reads, writes)
            else:
                s.op("dve", lambda e: e.tensor_scalar(out=out, in0=in0, scalar1=s1, scalar2=s2, op0=op0, op1=op1), reads, writes)

        def STT(out, in0, scalar, in1, op0, op1, reads, writes):
            s.op("dve", lambda e: e.scalar_tensor_tensor(out=out, in0=in0, scalar=scalar, in1=in1, op0=op0, op1=op1), reads, writes)

        def RECIP(out, in_, reads, writes):
            s.op("dve", lambda e: e.reciprocal(out=out, in_=in_), reads, writes)

        def COPY(eng, out, in_, reads, writes):
            s.op(eng, lambda e: e.tensor_copy(out=out, in_=in_), reads, writes)

        def MEMSET(eng, ap, val, reads, writes):
            s.op(eng, lambda e: e.memset(ap, val), reads, writes)

        def DMA(eng, out, in_, chan, reads, writes):
            s.dma(eng, lambda e: e.dma_start(out=out, in_=in_), chan, reads, writes)

        def chk(name, ap, keys):
            if dbg != name:
                return
            s.barrier()
            n = 1
            for d_ in ap.shape[1:]:
                n *= d_
            dst = dbg32 if ap.dtype == F32 else dbg16
            flat = dst[:, 0:n]
            if len(ap.shape) == 3:
                flat = flat.rearrange("p (a b) -> p a b", a=ap.shape[1])
            elif len(ap.shape) == 4:
                flat = flat.rearrange("p (a b c) -> p a b c", a=ap.shape[1], b=ap.shape[2])
            DMA("sp", flat, ap, "out", keys, [])
            raise _Stop()

        wq = []
        for l in range(DEPTH):
            base = l * NTILE_L
            wq += [base + i for i in range(1, 14)]
            for half in range(2):
                wq += [base + 14 + c for c in range(8)]
                wq += [base + 22, base + 23]
            wq += [base + 24 + i for i in range(16)]
        wstate = {"issued": 0, "used": 0}

        def w_issue(after=()):
            i = wstate["issued"]
            if i >= len(wq):
                return
            slot = i % NSLOT
            t = wq[i]
            n = TILE_N[t % NTILE_L]
            DMA("pool", wring[slot][:, 0:n], ws_d[t, :, 0:n], "w%d" % slot, list(after), [("w", slot)])
            wstate["issued"] += 1

        def w_next(expect):
            i = wstate["used"]
            wstate["used"] += 1
            while wstate["issued"] < min(len(wq), i + NSLOT - 1):
                w_issue()
            assert wq[i] % NTILE_L == expect, (wq[i], expect)
            return wring[i % NSLOT], ("w", i % NSLOT)

        DMA("sp", cb[:], cb_d, "c_cb", [], ["cb"])
        DMA("sp", gcol[:], g_d, "c_g", [], ["gcol"])
        DMA("sp", memT, memT_d.rearrange("(c p) t -> p c t", p=128), "mem", [], ["memscr_t"])
        DMA("pool", oflat[:, 0:4096], ws_d[0, :, 0:4096], "wmk", [], ["memscr_w"])
        DMA("sp", rope[:], rope_d, "c_rope", [], ["rope"])
        xv = xT_d.rearrange("(c p) t -> p c t", p=128)
        for tt in range(NT):
            DMA("sp", xT[:, :, tt * 512:(tt + 1) * 512], xv[:, :, tt * 512:(tt + 1) * 512], "x%d" % tt,
                [("xld", tt - 1)] if tt else [], [("x", c, tt) for c in range(8)] + [("xld", tt)])
        MEMSET("pool", vm[:, :, :, 64:128], 1.0, [], ["vmones"])
        for i in range(NBT):
            MEMSET("pool", bt[i][:, 0:64], 0.0, [], [("bt0", i)])
        for i in range(NSLOT):
            w_issue(after=[("xld", 2)])

        def rms_stats(src, src_keys, tt, n):
            sl = slice(tt * 512, tt * 512 + n)
            bk = bank("G")
            for c in range(8):
                si = nxt("sq", 2)
                ACT(sq[si][:, 0:n], src[:, c, sl], AF.Square, src_keys(c, tt), [("sq", si)])
                MM(banks[bk][:, 0:n], ones, sq[si][:, 0:n], c == 0, c == 7, [("sq", si), "cb"], [("ps", bk)])
            ta = nxt("tmpA", 2)
            tb = nxt("tmpB", 2)
            TS(tmpA[ta][:, 0:n], banks[bk][:, 0:n], 1.0 / D, EPS, ALU.mult, ALU.add, [("ps", bk)], [("tmpA", ta)])
            ACT(tmpA[ta][:, 0:n], tmpA[ta][:, 0:n], AF.Ln, [("tmpA", ta)], [("tmpA", ta)])
            ACT(tmpB[tb][:, 0:n], tmpA[ta][:, 0:n], AF.Exp, [("tmpA", ta)], [("tmpB", tb)], scale=-0.5)
            return tb

        def rmsnorm(src, src_keys, dst, dst_keys, gidx, ntok):
            nt = (ntok + 511) // 512
            for tt in range(nt):
                n = min(512, ntok - tt * 512)
                sl = slice(tt * 512, tt * 512 + n)
                tb = rms_stats(src, src_keys, tt, n)
                for c in range(8):
                    STT(dst[:, c, sl], src[:, c, sl], gcol[:, gidx * 8 + c:gidx * 8 + c + 1], tmpB[tb][:, 0:n], ALU.mult, ALU.mult,
                        src_keys(c, tt) + [("tmpB", tb), "gcol"], dst_keys(c, tt))

        from collections import deque
        bgq = deque()

        def bg_step():
            if bgq:
                try:
                    next(bgq[0][1])
                except StopIteration:
                    bgq.popleft()

        def bg_flush(tt=None):
            while bgq and (tt is None or any(t == tt for t, _ in bgq)):
                bg_step()

        ostg = oflat.bitcast(F32)
        ov = out_d.rearrange("(c p) t -> p c t", p=128)

        def norm_gen(tt, gidx, final=False):
            sl = slice(tt * 512, (tt + 1) * 512)
            bk = bank("N")
            for c in range(8):
                si = nxt("sq", 2)
                ACT(sq[si][:, :], xT[:, c, sl], AF.Square, [("x", c, tt)], [("sq", si)])
                MM(banks[bk][:, :], ones, sq[si][:, :], c == 0, c == 7, [("sq", si), "cb"], [("ps", bk)])
                yield
            tb = nxt("tmpB", 2)
            TS(tmpB[tb][:, :], banks[bk][:, :], 1.0 / D, EPS, ALU.mult, ALU.add, [("ps", bk)], [("tmpB", tb)])
            ACT(tmpB[tb][:, :], tmpB[tb][:, :], AF.Ln, [("tmpB", tb)], [("tmpB", tb)])
            ACT(tmpB[tb][:, :], tmpB[tb][:, :], AF.Exp, [("tmpB", tb)], [("tmpB", tb)], scale=-0.5)
            yield
            for c in range(8):
                if final:
                    so = nxt("ost", 12)
                    STT(ostg[:, so * 512:(so + 1) * 512], xT[:, c, sl], gcol[:, 48 + c:49 + c], tmpB[tb][:, :], ALU.mult, ALU.mult,
                        [("x", c, tt), ("tmpB", tb), "gcol"], [("ost", so)])
                    DMA("sp", ov[:, c, sl], ostg[:, so * 512:(so + 1) * 512], "out%d" % so, [("ost", so)], [])
                else:
                    STT(hT[:, c, sl], xT[:, c, sl], gcol[:, gidx * 8 + c:gidx * 8 + c + 1], tmpB[tb][:, :], ALU.mult, ALU.mult,
                        [("x", c, tt), ("tmpB", tb), "gcol"], [("h", c, tt)])
                if c % 2 == 1:
                    yield

        def mem_prep_dma(l):
            DMA("pool", oflat[:, 0:4096], ws_d[l * NTILE_L, :, 0:4096], "wmk", [], ["memscr_w"])
            DMA("sp", memT, memT_d.rearrange("(c p) t -> p c t", p=128), "mem", [], ["memscr_t"])

        def mem_prep_gen(l, issue=True):
            if issue:
                mem_prep_dma(l)
                for _ in range(24):
                    yield
            bk = bank("N")
            for c in range(8):
                si = nxt("sq", 2)
                ACT(sq[si][:, 0:256], memT[:, c, :], AF.Square, ["memscr_t"], [("sq", si)])
                MM(banks[bk][:, 0:256], ones, sq[si][:, 0:256], c == 0, c == 7, [("sq", si), "cb"], [("ps", bk)])
                yield
            tb = nxt("tmpB", 2)
            TS(tmpB[tb][:, 0:256], banks[bk][:, 0:256], 1.0 / D, EPS, ALU.mult, ALU.add, [("ps", bk)], [("tmpB", tb)])
            ACT(tmpB[tb][:, 0:256], tmpB[tb][:, 0:256], AF.Ln, [("tmpB", tb)], [("tmpB", tb)])
            ACT(tmpB[tb][:, 0:256], tmpB[tb][:, 0:256], AF.Exp, [("tmpB", tb)], [("tmpB", tb)], scale=-0.5)
            yield
            gidx = l * 3 + 1
            for c in range(8):
                STT(memn[:, c, :], memT[:, c, :], gcol[:, gidx * 8 + c:gidx * 8 + c + 1], tmpB[tb][:, 0:256], ALU.mult, ALU.mult,
                    ["memscr_t", ("tmpB", tb), "gcol"], ["memscr_n"])
                if c % 4 == 3:
                    yield
            for ch in range(2):
                bk = bank("N")
                for kc in range(8):
                    MM(banks[bk][:, 0:256], mkvw[:, kc, ch * 128:(ch + 1) * 128], memn[:, kc, :], kc == 0, kc == 7, ["memscr_n", "memscr_w"], [("ps", bk)])
                ACT(kmh[0:64, 2 * ch, :], banks[bk][0:64, 0:256], AF.Copy, [("ps", bk)], [("kmh", 2 * ch)])
                COPY("dve", kmh[0:64, 2 * ch + 1, :], banks[bk][64:128, 0:256], [("ps", bk)], [("kmh", 2 * ch + 1)])
                yield
            for mt in range(2):
                bk = bank("N")
                for kc in range(8):
                    MM(banks[bk][:, 0:256], memn[:, kc, mt * 128:(mt + 1) * 128], mkvw[:, kc, 256:512], kc == 0, kc == 7, ["memscr_n", "memscr_w"], [("ps", bk)])
                ACT(vm[:, mt, :, 0:64], banks[bk][:, 0:256].rearrange("p (h d) -> p h d", h=4), AF.Copy, [("ps", bk)], [("vm", mt)])
                yield

        def norm_tile(tt, gidx):
            bgq.append((tt, norm_gen(tt, gidx)))

        def final_tile(tt):
            bgq.append((tt, norm_gen(tt, 0, final=True)))

        xk = lambda c, tt: [("x", c, tt)]
        hk = lambda c, tt: [("h", c, tt)]
        hall = [("h", c, tt) for c in range(8) for tt in range(NT)]

        def dense(bk, n, lhs_list, rhs_list, reads):
            nk = len(lhs_list)
            for kc in range(nk):
                MM(banks[bk][:, 0:n], lhs_list[kc], rhs_list[kc], kc == 0, kc == nk - 1, reads, [("ps", bk)])
            bg_step()

        def produce_qk(wt, wkey, d=1):
            wv = wt[:, 0:2048].rearrange("p (k c) -> p k c", k=8)
            for tt in range(NT):
                tsl = slice(tt * 512, (tt + 1) * 512)
                hkeys = [("h", c, tt) for c in range(8)]
                rhs = [hT[:, kc, tsl] for kc in range(8)]
                items = []
                for which, dst, dkey in ((0, qh, "qh"), (1, kh, "kh")):
                    bA = bank("S6")
                    dense(bA, 512, [wv[:, kc, which * 128:(which + 1) * 128] for kc in range(8)], rhs, [wkey] + hkeys)
                    si = nxt("sq", 2)
                    ACT(sq[si][:], banks[bA][:, :], AF.Identity, [("ps", bA)], [("sq", si)])
                    items.append((dst, dkey, bA, si))
                for dst, dkey, bA, si in items:
                    bB = bank("P")
                    MM(banks[bB][:, :], pmat, sq[si][:], True, True, [("sq", si), "cb"], [("ps", bB)])
                    ta = nxt("tmpA", 2)
                    tb = nxt("tmpB", 2)
                    if d == 1:
                        TT("dve", tmpA[ta][:], banks[bA][:, :], rope[:, 0, tsl], ALU.mult, [("ps", bA), "rope"], [("tmpA", ta)])
                        TT("dve", tmpB[tb][:], banks[bB][:, :], rope[:, 1, tsl], ALU.mult, [("ps", bB), "rope"], [("tmpB", tb)])
                    else:
                        pv = lambda ap: ap.rearrange("p (m r) -> p m r", r=d)
                        pw = lambda ap: ap.rearrange("p (r m) -> p m r", r=d)
                        TT("dve", pw(tmpA[ta][:, :]), pv(banks[bA][:, :]), pv(rope[:, 0, tsl]), ALU.mult, [("ps", bA), "rope"], [("tmpA", ta)])
                        TT("dve", pw(tmpB[tb][:, :]), pv(banks[bB][:, :]), pv(rope[:, 1, tsl]), ALU.mult, [("ps", bB), "rope"], [("tmpB", tb)])
                    for hd in range(2):
                        rows = slice(hd * 64, (hd + 1) * 64)
                        if d == 1:
                            o_, a_, b_ = dst[hd][0:64, tsl], tmpA[ta][rows, :], tmpB[tb][rows, :]
                        else:
                            m0, m1 = tt * 512 // d, (tt + 1) * 512 // d
                            o_ = dst[hd][0:64, :].rearrange("p (r m) -> p r m", r=d)[:, :, m0:m1]
                            a_ = tmpA[ta][rows, :].rearrange("p (r m) -> p r m", r=d)
                            b_ = tmpB[tb][rows, :].rearrange("p (r m) -> p r m", r=d)
                        TT("pool" if hd == 0 else "dve", o_, a_, b_, ALU.add, [("tmpA", ta), ("tmpB", tb)], [(dkey, hd, tt)])

        def produce_q_plain(wt, wkey, col0):
            wv = wt[:, 0:2048].rearrange("p (k c) -> p k c", k=8)
            for tt in range(NT):
                tsl = slice(tt * 512, (tt + 1) * 512)
                hkeys = [("h", c, tt) for c in range(8)]
                bA = bank("S")
                dense(bA, 512, [wv[:, kc, col0:col0 + 128] for kc in range(8)], [hT[:, kc, tsl] for kc in range(8)], [wkey] + hkeys)
                ACT(qh[0][0:64, tsl], banks[bA][0:64, :], AF.Copy, [("ps", bA)], [("qh", 0, tt)])
                COPY("dve", qh[1][0:64, tsl], banks[bA][64:128, :], [("ps", bA)], [("qh", 1, tt)])

        def produce_v(wt, wkey, tok_slices):
            wv = wt[:, 0:1024].rearrange("p (k c) -> p k c", k=8)
            for t0 in range(0, 16, 4):
                bk = bank("G")
                for ti in range(4):
                    tsl = tok_slices[t0 + ti]
                    for kc in range(8):
                        MM(banks[bk][:, ti * 128:(ti + 1) * 128], hT[:, kc, tsl], wv[:, kc, :], kc == 0, kc == 7, [wkey] + hall, [("ps", bk)])
                ACT(vt[:, t0:t0 + 4, :, 0:64], banks[bk][:, :].rearrange("p (t h d) -> p t h d", t=4, h=2), AF.Copy,
                    [("ps", bk)], [("v", t0 + i) for i in range(4)])

        def finish_o(ob, n, dst_chunk, hd, tok0):
            tc_ = nxt("tmpC", 2)
            ACT(tmpC[tc_][64:128, 0:n], banks[ob][64:128, 0:n], AF.Ln, [("ps", ob)], [("tmpC", tc_)])
            ACT(tmpC[tc_][64:128, 0:n], tmpC[tc_][64:128, 0:n], AF.Exp, [("tmpC", tc_)], [("tmpC", tc_)], scale=-1.0)
            wk = [("o", dst_chunk, hd)] + (["accA"] if 1 <= dst_chunk <= 4 else [])
            TT("dve", oT[hd * 64:(hd + 1) * 64, dst_chunk, tok0:tok0 + n], banks[ob][0:64, 0:n], tmpC[tc_][64:128, 0:n], ALU.mult,
               [("ps", ob), ("tmpC", tc_)], wk)

        def run_tiles(tiles):
            LOOK = 3
            for i in range(len(tiles) + LOOK):
                if i < len(tiles):
                    t = tiles[i]
                    sbk = bank("S")
                    t["s"](sbk)
                    pi = nxt("pT", NPT)
                    n = t["n"]
                    ACT(pT[pi][:, 0:n], banks[sbk][:, 0:n], AF.Exp, [("ps", sbk)], [("pT", pi)], scale=0.125)
                    t["pi"] = pi
                j = i - LOOK
                if 0 <= j < len(tiles):
                    tiles[j]["pv"](tiles[j])

        try:
          for l in range(DEPTH):
              chk('h', hT[:], hall)

              if l == 0:
                  for _ in mem_prep_gen(0, issue=False):
                      pass
              MEMSET("pool", vt[:, :, :, 64:128], 1.0, [], VKEYS)
              if l == 0:
                  rmsnorm(xT, xk, hT, hk, 0, S)
              MEMSET("pool", accA, 0.0, [], ["accA", "memscr_t", "memscr_n", "memscr_w"])
              for g in range(3):
                  d = DIL[g]
                  nblk = 16 // d
                  wV, wVk = w_next(1 + 2 * g)
                  wR, wRk = w_next(2 + 2 * g)
                  toks = []
                  for r in range(d):
                      for j in range(nblk):
                          st = j * 128 * d + r
                          toks.append(slice(st, st + 127 * d + 1, d))
                  produce_qk(wR, wRk, d)
                  produce_v(wV, wVk, toks)
                  if g == 0:
                      chk('qkA0', qk[:], [])
                      chk('vA0', vt[:], [])
                  if g == 1:
                      chk('qkA1', qk[:], [])
                      chk('vA1', vt[:], [])
                  tiles = []
                  for hd in range(2):
                      qkeys = [("qh", hd, tt) for tt in range(NT)]
                      kkeys = [("kh", hd, tt) for tt in range(NT)]
                      for r in range(d):
                          ostate = {}
                          for jk in range(nblk):
                              nq = 256 if jk < nblk - 1 else 128
                              st = r * (S // d) + jk * 128
                              ksl = slice(st, st + 128)
                              qsl = slice(st, st + nq)

                              def sfn(sbk, hd=hd, ksl=ksl, qsl=qsl, nq=nq, qkeys=qkeys, kkeys=kkeys):
                                  MM(banks[sbk][:, 0:nq], kh[hd][0:64, ksl], qh[hd][0:64, qsl], True, True, qkeys + kkeys, [("ps", sbk)])
                                  MM(banks[sbk][:, 0:nq], ident, maskA[:, 0:nq], False, True, ["cb"], [("ps", sbk)], skip=True)

                              def pvfn(t, hd=hd, r=r, jk=jk, d=d, nblk=nblk, ostate=ostate):
                                  gi = jk % 4
                                  if gi == 0:
                                      ostate["ob"] = bank("O")
                                  ob = ostate["ob"]
                                  vi = r * nblk + jk
                                  osl = banks[ob][:, gi * 128:(gi + 1) * 128]
                                  if jk > 0:
                                      pprev = ostate["prev_pi"]
                                      MM(osl, vt[:, vi - 1, hd, :], pT[pprev][:, 128:256], True, False,
                                         [("pT", pprev), ("v", vi - 1), "vones"], [("ps", ob)], skip=True)
                                  pc = t["pi"]
                                  MM(osl, vt[:, vi, hd, :], pT[pc][:, 0:128], jk == 0, True, [("pT", pc), ("v", vi), "vones"], [("ps", ob)], skip=True)
                                  ostate["prev_pi"] = pc
                                  if gi == 3 or jk == nblk - 1:
                                      j0 = jk - gi
                                      n = (gi + 1) * 128
                                      st0 = j0 * 128 * d + r
                                      asl = slice(st0, st0 + (n - 1) * d + 1, d)
                                      TT("dve", accA[:, hd, asl], banks[ob][:, 0:n], accA[:, hd, asl], ALU.add, [("ps", ob), "accA"], ["accA"])

                              tiles.append({"s": sfn, "n": nq, "pv": pvfn})
                  run_tiles(tiles)
              def a_final_gen():
                  for hd in range(2):
                      for tt in range(NT):
                          tsl = slice(tt * 512, (tt + 1) * 512)
                          tc_ = nxt("tmpC", 2)
                          ACT(tmpC[tc_][64:128, :], accA[64:128, hd, tsl], AF.Ln, ["accA"], [("tmpC", tc_)])
                          ACT(tmpC[tc_][64:128, :], tmpC[tc_][64:128, :], AF.Exp, [("tmpC", tc_)], [("tmpC", tc_)], scale=-1.0)
                          COPY("dve", tmpC[tc_][0:64, :], tmpC[tc_][64:128, :], [("tmpC", tc_)], [("tmpC", tc_)])
                          TT("dve", oT[hd * 64:(hd + 1) * 64, 0, tsl], accA[0:64, hd, tsl], tmpC[tc_][0:64, :], ALU.mult,
                             ["accA", ("tmpC", tc_)], [("o", 0, hd), "memscr_w"])
                          yield
              bgq.append((-1, a_final_gen()))
              if dbg == 'oA':
                  bg_flush()

              chk('oA', oT[:, 0, :], [])
              for i in range(2):
                  DMA("sp", kh[i][64:72, :], kaug_d, "ka%d" % i, [], [("kaug", i)])
              nat = [slice(t * 128, (t + 1) * 128) for t in range(16)]
              for g in range(3):
                  wV, wVk = w_next(7 + 2 * g)
                  wR, wRk = w_next(8 + 2 * g)
                  produce_qk(wR, wRk)
                  produce_v(wV, wVk, nat)
                  for hd in range(2):
                      kk = [("kh", hd, tt) for tt in range(NT)]
                      s.op("dve", lambda e, hd=hd: e.tensor_reduce(out=km2[hd][0:64, :], in_=kh[hd][0:64, :].rearrange("p (c t) -> p c t", t=256),
                                                                    axis=AX.X, op=ALU.add), kk, [("km", hd)])
                      TS(kmb2[hd][0:64, :], km2[hd][0:64, :], 1.0 / 256, None, ALU.mult, None, [("km", hd)], [("kmb", hd)])
                  for hd in range(2):
                      qk_ = [("qh", hd, tt) for tt in range(NT)]
                      gb = bank("G")
                      for tq in range(8, 16):
                          MM(banks[gb][:, tq * 8:(tq + 1) * 8], qh[hd][0:64, tq * 128:(tq + 1) * 128], kmb2[hd][0:64, :], True, True,
                             qk_ + [("kmb", hd)], [("ps", gb)])
                      COPY("dve", gsb2[hd][:, 8:16, :].rearrange("p a b -> p (a b)"), banks[gb][:, 64:128], [("ps", gb)], [("gsb", hd)])
                      MEMSET("pool", qh[hd][64:72, 0:1024], 0.0, [], [("qa", hd, 0), ("qa", hd, 1)])
                  btidx = {}
                  for hd in range(2):
                      for b in range(4, 8):
                          MEMSET("dve", gsb2[hd][:, 2 * b:2 * b + 2, b:8], 1e30, [("gsb", hd)], [("gsb", hd)])
                      for tq in range(8, 16):
                          b = tq // 2
                          bi = nxt("bt", NBT)
                          btidx[(hd, tq)] = bi
                          s.op("dve", lambda e, tq=tq, hd=hd: e.max(out=m8[:], in_=gsb2[hd][:, tq, :]), [("gsb", hd)], ["m8"])
                          TS(bt[bi][:, 64:72], gsb2[hd][:, tq, :], m8[:, 10 - b:11 - b], 1.0, ALU.is_ge, ALU.subtract, [("gsb", hd), "m8"], [("bt", bi)])

                  def moba_tiles(hd, qts, g=g):
                      tiles = []
                      for qt in qts:
                          ostate = {}
                          last = 4 * qt + 3
                          for kt in range(last + 1):
                              q0 = max(kt * 128, qt * 512)
                              n = (qt + 1) * 512 - q0
                              diag = kt >= 4 * qt

                              def sfn(sbk, hd=hd, kt=kt, q0=q0, n=n, diag=diag, qt=qt):
                                  MM(banks[sbk][:, 0:n], kh[hd][0:72, kt * 128:(kt + 1) * 128], qh[hd][0:72, q0:q0 + n], True, True,
                                     [("kh", hd, kt // 4), ("kaug", hd), ("qh", hd, qt), ("qa", hd, qt)], [("ps", sbk)])
                                  if diag:
                                      MM(banks[sbk][:, 0:128], ident, caus, False, True, ["cb"], [("ps", sbk)], skip=True)

                              def pvfn(t, hd=hd, kt=kt, q0=q0, n=n, qt=qt, last=last, ostate=ostate, g=g):
                                  if kt == 0:
                                      ostate["ob"] = bank("O")
                                  ob = ostate["ob"]
                                  c0 = q0 - qt * 512
                                  pc = t["pi"]
                                  MM(banks[ob][:, c0:512], vt[:, kt, hd, :], pT[pc][:, 0:n], kt == 0, kt == last,
                                     [("pT", pc), ("v", kt), "vones"], [("ps", ob)], skip=True)
                                  if kt == last:
                                      finish_o(ob, 512, 1 + g, hd, qt * 512)

                              tiles.append({"s": sfn, "n": n, "pv": pvfn})
                      return tiles

                  bg_flush()
                  run_tiles(moba_tiles(0, (0, 1)) + moba_tiles(1, (0, 1)))
                  for hd in range(2):
                      for tq4 in (2, 3):
                          tb_ = bank("G")
                          for ti in range(4):
                              bi = btidx[(hd, tq4 * 4 + ti)]
                              MM(banks[tb_][0:72, ti * 128:(ti + 1) * 128], bt[bi][:, 0:72], ident, True, True,
                                 [("bt", bi), ("bt0", bi), "cb"], [("ps", tb_)])
                          ACT(qh[hd][64:72, tq4 * 512:(tq4 + 1) * 512], banks[tb_][64:72, :], AF.Copy, [("ps", tb_)], [("qa", hd, tq4)])
                  run_tiles(moba_tiles(0, (2, 3)) + moba_tiles(1, (2, 3)))

              chk('oB', oT[:, 1:4, :], [])
              wQ, wQk = w_next(13)
              for ch in range(2):
                  produce_q_plain(wQ, wQk, ch * 128)
                  tiles = []
                  for hd in range(2):
                      hm = 2 * ch + hd
                      for qt in range(NT):
                          ostate = {}
                          for mt in range(2):
                              def sfn(sbk, hd=hd, hm=hm, qt=qt, mt=mt):
                                  MM(banks[sbk][:, 0:512], kmh[0:64, hm, mt * 128:(mt + 1) * 128], qh[hd][0:64, qt * 512:(qt + 1) * 512], True, True,
                                     [("kmh", hm), ("qh", hd, qt)], [("ps", sbk)])

                              def pvfn(t, hd=hd, hm=hm, qt=qt, mt=mt, ostate=ostate, ch=ch):
                                  if mt == 0:
                                      ostate["ob"] = bank("O")
                                  ob = ostate["ob"]
                                  pc = t["pi"]
                                  MM(banks[ob][:, 0:512], vm[:, mt, hm, :], pT[pc][:, 0:512], mt == 0, mt == 1,
                                     [("pT", pc), ("vm", mt), "vmones"], [("ps", ob)])
                                  if mt == 1:
                                      finish_o(ob, 512, 4 + ch, hd, qt * 512)

                              tiles.append({"s": sfn, "n": 512, "pv": pvfn})
                  run_tiles(tiles)

              chk('oM', oT[:, 4:6, :], [])
              s.barrier()
              okeys = [("o", ci, hd) for ci in range(6) for hd in range(2)]
              for half in range(2):
                  for c in range(8):
                      wY, wYk = w_next(14 + c)
                      wG = wY[:, 0:3072].rearrange("p (k c) -> p k c", k=8)
                      wP = wY[:, 3072:3840].rearrange("p (k c) -> p k c", k=6)
                      for t2 in range(2):
                          tt = half * 2 + t2
                          tsl = slice(tt * 512, (tt + 1) * 512)
                          ysl = slice(t2 * 512, (t2 + 1) * 512)
                          hkeys = [("h", cc, tt) for cc in range(8)]
                          rhs_h = [hT[:, kc, tsl] for kc in range(8)]
                          prods = []
                          for i, (k0, nk) in enumerate(((0, 1), (1, 3), (4, 2))):
                              bg = bank("S")
                              dense(bg, 512, [wG[:, kc, i * 128:(i + 1) * 128] for kc in range(8)], rhs_h, [wYk] + hkeys)
                              if i == 1:
                                  ti_ = nxt("tmpC", 2)
                                  sg, sgk = tmpC[ti_], ("tmpC", ti_)
                              else:
                                  ti_ = nxt("tmpA", 2)
                                  sg, sgk = tmpA[ti_], ("tmpA", ti_)
                              ACT(sg[:], banks[bg][:, :], AF.Sigmoid, [("ps", bg)], [sgk])
                              bp = bank("S")
                              dense(bp, 512, [wP[:, k0 + kc, :] for kc in range(nk)], [oT[:, k0 + kc, tsl] for kc in range(nk)], [wYk] + okeys)
                              TT("dve", sg[:], banks[bp][:, :], sg[:], ALU.mult, [("ps", bp), sgk], [sgk])
                              prods.append((sg, sgk))
                          TT("pool", prods[0][0][:], prods[0][0][:], prods[1][0][:], ALU.add, [prods[0][1], prods[1][1]], [prods[0][1]])
                          TT("pool", yT[:, c, ysl], prods[0][0][:], prods[2][0][:], ALU.add, [prods[0][1], prods[2][1]], [("y", c, t2)])
                  for oh in range(2):
                      wO, wOk = w_next(22 + oh)
                      wOv = wO[:, 0:4096].rearrange("p (k c) -> p k c", k=8)
                      for cc in range(4):
                          co = oh * 4 + cc
                          for t2 in range(2):
                              tt = half * 2 + t2
                              tsl = slice(tt * 512, (tt + 1) * 512)
                              ysl = slice(t2 * 512, (t2 + 1) * 512)
                              bo = bank("S")
                              dense(bo, 512, [wOv[:, kc, cc * 128:(cc + 1) * 128] for kc in range(8)], [yT[:, kc, ysl] for kc in range(8)],
                                    [wOk] + [("y", kc, t2) for kc in range(8)])
                              TT("dve", xT[:, co, tsl], banks[bo][:, :], xT[:, co, tsl], ALU.add, [("ps", bo), ("x", co, tt)], [("x", co, tt)])
                  for t2 in range(2):
                      norm_tile(half * 2 + t2, l * 3 + 2)
              s.barrier()

              if dbg == 'x1':
                  bg_flush()
              chk('x1', xT[:], [])
              if l + 1 < DEPTH and not dbg:
                  bgq.append((-2, mem_prep_gen(l + 1)))
              seq = [(g, tt) for g in range(8) for tt in range(NT)]
              wst = {}
              prev = None
              for item in seq + [None]:
                  if item is not None:
                      g, tt = item
                      if g == 0:
                          bg_flush(tt)
                      if tt == 0:
                          wU, wUk = w_next(24 + 2 * g)
                          wst["U"] = (wU[:, 0:4096].rearrange("p (k c) -> p k c", k=8), wUk)
                      wUv, wUk = wst["U"]
                      tsl = slice(tt * 512, (tt + 1) * 512)
                      hkeys = [("h", cc, tt) for cc in range(8)]
                      rhs_h = [hT[:, kc, tsl] for kc in range(8)]
                      ui = nxt("u", 2)
                      for j in range(4):
                          bu = bank("S")
                          dense(bu, 512, [wUv[:, kc, j * 128:(j + 1) * 128] for kc in range(8)], rhs_h, [wUk] + hkeys)
                          ri = nxt("r", 2)
                          ACT(rT[ri], banks[bu][:, :], AF.Relu, [("ps", bu)], [("r", ri)])
                          TT("pool", uT[ui][:, j, :], rT[ri], rT[ri], ALU.mult, [("r", ri)], [("u", ui, j)])
                      cur = (g, tt, ui)
                      if g == 7 and tt == NT - 1:
                          w_issue()
                  else:
                      cur = None
                  if prev is not None:
                      g, tt, ui = prev
                      if tt == 0:
                          wD, wDk = w_next(25 + 2 * g)
                          wst["D"] = (wD[:, 0:4096].rearrange("p (j c) -> p j c", j=4), wDk)
                      wDv, wDk = wst["D"]
                      tsl = slice(tt * 512, (tt + 1) * 512)
                      for co in range(8):
                          bd = bank("D3")
                          dense(bd, 512, [wDv[:, jj, co * 128:(co + 1) * 128] for jj in range(4)], [uT[ui][:, jj, :] for jj in range(4)],
                                [wDk] + [("u", ui, jj) for jj in range(4)])
                          TT("dve", xT[:, co, tsl], banks[bd][:, :], xT[:, co, tsl], ALU.add, [("ps", bd), ("x", co, tt)], [("x", co, tt)])
                      if g == 7 and not dbg:
                          if l + 1 < DEPTH:
                              norm_tile(tt, (l + 1) * 3)
                          else:
                              final_tile(tt)
                  prev = cur
              w_issue()
              bg_flush()
              s.barrier()
              chk('x2', xT[:], [])

        except _Stop:
            pass
        else:
          assert wstate["used"] == len(wq), (wstate, len(wq))
        s.emit(nc)
    return nc


_NC_CACHE = {}


def kernel(_dbg=None, **inp):
    x = np.asarray(inp["x"], np.float32)
    mem = np.asarray(inp["mem"], np.float32)
    B = x.shape[0]
    ws = _build_wstream(inp)
    cbn, ropen, kaugn = _consts()
    vecs = []
    for l in range(DEPTH):
        vecs += [inp["norm_mix"][l], inp["norm_mem"][l], inp["norm_mlp"][l]]
    vecs.append(inp["norm_final"])
    gcols = np.concatenate([np.asarray(v, np.float32).reshape(8, 128).T for v in vecs], axis=1)
    gcols = np.ascontiguousarray(gcols)
    if _dbg:
        nc = build_nc(_dbg)
    else:
        if "nc" not in _NC_CACHE:
            _NC_CACHE["nc"] = build_nc()
        nc = _NC_CACHE["nc"]
    in_maps = []
    for b in range(B):
        in_maps.append({
            "xT": np.ascontiguousarray(x[b].T),
            "memT": np.ascontiguousarray(mem[b].T),
            "wstream": ws,
            "gcols": gcols,
            "cb": cbn,
            "rope": ropen,
            "kaug": kaugn,
        })
    res = run_bass_kernel_spmd(nc, in_maps, core_ids=list(range(B)))
    if _dbg:
        return res.results[0]
    out = np.stack([np.ascontiguousarray(r["outT"].T) for r in res.results], 0)
    return out.astype(np.float32)
```
